# Optimizing a Trainium2 kernel written in Bass

```python
import jax, jax.numpy as jnp
from jax import lax
import numpy as np

D_MODEL = 1024
BATCH = 8
SEQ = 2048
DEPTH = 1
DEC_BATCH = 128
DEC_SEQ = 1
PAST_LEN = 16384
PAGE_SIZE = 128

H_R = 4
DK_R = 256
DV_R = 512
QR = H_R * DK_R
VR = H_R * DV_R
D_A = 2048
G_A = 8
CHUNK = 128
D_FF = ((8 * D_MODEL // 3 + 255) // 256) * 256
D_PLE = 256
ROPE_BASE = 10000.0
EPS = 1e-6
D_IN = 2 * QR + 2 * VR + 2 * D_A + 2 * D_MODEL
SPLITS = [QR, 2 * QR, 2 * QR + VR, 2 * QR + 2 * VR, 2 * QR + 2 * VR + D_A, 2 * QR + 2 * VR + 2 * D_A]

kernel_name = "hybrid_retention_chunkgmlp_decoder_step"


def rms_norm(x, g):
    x32 = x.astype(jnp.float32)
    y = x32 * lax.rsqrt(jnp.mean(x32 * x32, axis=-1, keepdims=True) + EPS) * g.astype(jnp.float32)
    return y.astype(x.dtype)


def layer_norm(x, g, b):
    x32 = x.astype(jnp.float32)
    mu = jnp.mean(x32, axis=-1, keepdims=True)
    var = jnp.mean(jnp.square(x32 - mu), axis=-1, keepdims=True)
    return (x32 - mu) * lax.rsqrt(var + EPS) * g.astype(jnp.float32) + b.astype(jnp.float32)


def rope(x, pos):
    half = x.shape[-1] // 2
    freqs = ROPE_BASE ** (-jnp.arange(half, dtype=jnp.float32) / half)
    ang = pos[:, None] * freqs[None, :]
    cos = jnp.cos(ang)[None, :, None, :]
    sin = jnp.sin(ang)[None, :, None, :]
    x1, x2 = x[..., :half], x[..., half:]
    return jnp.concatenate([x1 * cos - x2 * sin, x1 * sin + x2 * cos], axis=-1)


def retention(q, k, v, s0):
    B, T = q.shape[0], q.shape[1]
    L = min(T, CHUNK)
    n = T // L
    lg = jnp.log1p(-jnp.exp2(-5.0 - jnp.arange(H_R, dtype=jnp.float32)))
    idx = jnp.arange(L, dtype=jnp.float32)
    diff = idx[:, None] - idx[None, :]
    decay = jnp.where(diff >= 0, jnp.exp(lg[:, None, None] * jnp.maximum(diff, 0.0)), 0.0)
    q_dec = jnp.exp(lg[None, :] * (idx[:, None] + 1.0))
    k_dec = jnp.exp(lg[None, :] * (L - 1.0 - idx[:, None]))
    s_dec = jnp.exp(lg * L)

    def to_chunks(a):
        return a.reshape(B, n, L, *a.shape[2:]).swapaxes(0, 1)

    def step(s, qkv):
        qc, kc, vc = qkv
        scores = jnp.einsum('bihd,bjhd->bhij', qc, kc) * decay[None]
        o = (jnp.einsum('bhij,bjhv->bihv', scores, vc)
             + jnp.einsum('bihd,bhdv->bihv', qc, s) * q_dec[None, :, :, None])
        s = s * s_dec[None, :, None, None] + jnp.einsum('bjhd,bjhv->bhdv', kc * k_dec[None, :, :, None], vc)
        return s, o

    s_fin, o = lax.scan(step, s0.astype(jnp.float32), (to_chunks(q), to_chunks(k), to_chunks(v)))
    return o.swapaxes(0, 1).reshape(B, T, H_R, DV_R), s_fin


def chunk_spatial_mix(va, w_s, b_s):
    B, T = va.shape[0], va.shape[1]
    L = min(T, CHUNK)
    n = T // L
    v5 = va.reshape(B, n, L, G_A, D_A // G_A)
    w = w_s[:, :L, :L] * jnp.tril(jnp.ones((L, L), w_s.dtype))[None]
    s = jnp.einsum('gij,bnjgc->bnigc', w, v5) + b_s[:, :L].T[None, None, :, :, None]
    return s.reshape(B, T, D_A)


def layer(x, p, pos, s0, g_mix, w_in, w_ret_out, ln_a_g, ln_a_b, w_s, b_s, w_a_out, b_gate, w_o,
          g_ffn, w_ff_gate, w_ff_up, w_ff_down, g_ple, w_ple, w_ple_gate):
    B, T = x.shape[0], x.shape[1]
    xn = rms_norm(x, g_mix)
    z = xn @ w_in
    q_r, k_r, v_r, g_r, u_a, v_a, gates = jnp.split(z, SPLITS, axis=-1)
    q = rope(q_r.reshape(B, T, H_R, DK_R).astype(jnp.float32), pos)
    k = rope(k_r.reshape(B, T, H_R, DK_R).astype(jnp.float32), pos) * (DK_R ** -0.5)
    v = v_r.reshape(B, T, H_R, DV_R).astype(jnp.float32)
    o, s_new = retention(q, k, v, s0)
    mu = jnp.mean(o, axis=-1, keepdims=True)
    var = jnp.mean(jnp.square(o - mu), axis=-1, keepdims=True)
    o = ((o - mu) * lax.rsqrt(var + EPS)).reshape(B, T, VR)
    out_b = (jax.nn.silu(g_r.astype(jnp.float32)) * o).astype(x.dtype) @ w_ret_out
    u = jax.nn.gelu(u_a)
    va = layer_norm(jax.nn.gelu(v_a), ln_a_g, ln_a_b).astype(x.dtype)
    s_sp = chunk_spatial_mix(va, w_s, b_s)
    out_a = (u * s_sp) @ w_a_out
    gt = jax.nn.sigmoid(gates + b_gate)
    g_a, g_b = gt[..., :D_MODEL], gt[..., D_MODEL:]
    h = x + (g_a * out_a + g_b * out_b) @ w_o
    hn = rms_norm(h, g_ffn)
    h = h + (jax.nn.silu(hn @ w_ff_gate) * (hn @ w_ff_up)) @ w_ff_down
    h = h + jax.nn.sigmoid(rms_norm(h, g_ple) @ w_ple_gate) * (p @ w_ple)
    return h, s_new, va


def setup_inputs(seed: int = 0) -> dict:
    key = jax.random.key(seed)
    ks = jax.random.split(key, 32)
    f32 = jnp.float32

    def nrm(k, shape, scale):
        return jax.random.normal(k, shape, f32) * scale

    def gain(k, shape):
        return 1.0 + 0.01 * jax.random.normal(k, shape, f32)

    return {
        "x_prompt": nrm(ks[0], (BATCH, SEQ, D_MODEL), 1.0),
        "x_sample": nrm(ks[1], (DEC_BATCH, DEC_SEQ, D_MODEL), 1.0),
        "state_ret": nrm(ks[2], (DEPTH, DEC_BATCH, H_R, DK_R, DV_R), 0.1),
        "p_prompt": nrm(ks[3], (DEPTH, BATCH, SEQ, D_PLE), 1.0),
        "p_sample": nrm(ks[4], (DEPTH, DEC_BATCH, DEC_SEQ, D_PLE), 1.0),
        "g_mix": gain(ks[5], (DEPTH, D_MODEL)),
        "w_in": nrm(ks[6], (DEPTH, D_MODEL, D_IN), D_MODEL ** -0.5),
        "w_ret_out": nrm(ks[7], (DEPTH, VR, D_MODEL), VR ** -0.5),
        "ln_a_g": gain(ks[8], (DEPTH, D_A)),
        "ln_a_b": nrm(ks[9], (DEPTH, D_A), 0.01),
        "w_s": nrm(ks[10], (DEPTH, G_A, CHUNK, CHUNK), 0.5 * CHUNK ** -0.5),
        "b_s": gain(ks[11], (DEPTH, G_A, CHUNK)),
        "w_a_out": nrm(ks[12], (DEPTH, D_A, D_MODEL), D_A ** -0.5),
        "b_gate": nrm(ks[13], (DEPTH, 2 * D_MODEL), 0.01),
        "w_o": nrm(ks[14], (DEPTH, D_MODEL, D_MODEL), D_MODEL ** -0.5),
        "g_ffn": gain(ks[15], (DEPTH, D_MODEL)),
        "w_ff_gate": nrm(ks[16], (DEPTH, D_MODEL, D_FF), D_MODEL ** -0.5),
        "w_ff_up": nrm(ks[17], (DEPTH, D_MODEL, D_FF), D_MODEL ** -0.5),
        "w_ff_down": nrm(ks[18], (DEPTH, D_FF, D_MODEL), D_FF ** -0.5),
        "g_ple": gain(ks[19], (DEPTH, D_MODEL)),
        "w_ple": nrm(ks[20], (DEPTH, D_PLE, D_MODEL), D_PLE ** -0.5),
        "w_ple_gate": nrm(ks[21], (DEPTH, D_MODEL, D_MODEL), D_MODEL ** -0.5),
        "g_final": gain(ks[22], (D_MODEL,)),
    }


def reference(x_prompt, x_sample, state_ret, p_prompt, p_sample, g_mix, w_in, w_ret_out, ln_a_g, ln_a_b,
              w_s, b_s, w_a_out, b_gate, w_o, g_ffn, w_ff_gate, w_ff_up, w_ff_down, g_ple, w_ple,
              w_ple_gate, g_final):
    pos_p = jnp.arange(SEQ, dtype=jnp.float32)
    pos_s = PAST_LEN + jnp.arange(DEC_SEQ, dtype=jnp.float32)
    s0_prompt = jnp.zeros((BATCH, H_R, DK_R, DV_R), jnp.float32)
    h_p, h_s = x_prompt, x_sample
    st_p, st_s, vrows_s = [], [], []
    for i in range(DEPTH):
        lw = (g_mix[i], w_in[i], w_ret_out[i], ln_a_g[i], ln_a_b[i], w_s[i], b_s[i], w_a_out[i], b_gate[i],
              w_o[i], g_ffn[i], w_ff_gate[i], w_ff_up[i], w_ff_down[i], g_ple[i], w_ple[i], w_ple_gate[i])
        h_p, sp, _ = layer(h_p, p_prompt[i], pos_p, s0_prompt, *lw)
        h_s, ss, vs = layer(h_s, p_sample[i], pos_s, state_ret[i], *lw)
        st_p.append(sp)
        st_s.append(ss)
        vrows_s.append(vs)
    y_prompt = rms_norm(h_p, g_final)
    y_sample = rms_norm(h_s, g_final)
    state_ret_prompt = jnp.stack(st_p, axis=0)
    state_ret_sample = jnp.stack(st_s, axis=0)
    chunk_v_sample = jnp.stack(vrows_s, axis=0)
    return (y_prompt, y_sample, state_ret_prompt, state_ret_sample, chunk_v_sample)
```

```python
import numpy as np
from contextlib import ExitStack
import concourse.bass as bass
import concourse.mybir as mybir
from concourse.bass_utils import run_bass_kernel_spmd

F32 = mybir.dt.float32
BF16 = mybir.dt.bfloat16
AF = mybir.ActivationFunctionType
ALU = mybir.AluOpType
AX = mybir.AxisListType

D = 1024
SEQ = 2048
NS = 16
H = 4
DK = 256
DV = 512
VR = 2048
DA = 2048
GA = 8
DFF = 2816
DPLE = 256
DIN = 12288
EPS = 1e-6
PAST = 16384
TB = 512
NT = TB // 128
NB = SEQ // TB
NSLOT = 5
PF = 3
GAM = [float(1.0 - 2.0 ** (-5.0 - h)) for h in range(H)]
GAML = [float(np.float32(g) ** 128) for g in GAM]

ENGS = ("tensor", "vector", "scalar", "gpsimd", "sync")
SEM_ROT = 30000


class Buf:
    __slots__ = ("name", "w", "rs", "psum", "dsem", "osem", "free")

    def __init__(self, name, psum=False):
        self.name = name
        self.w = None
        self.rs = []
        self.psum = psum
        self.dsem = None
        self.osem = None
        self.free = True


def I(method, *args, **kw):
    return lambda e: getattr(e, method)(*args, **kw)


class FW:
    def __init__(self, nc, stack):
        self.nc = nc
        self.stack = stack
        self.ops = {e: [] for e in ENGS}
        self.cur = {}
        self.nsem = 0
        for e in ENGS:
            self.cur[e] = [self._newsem(e), 0]
        self.known = {e: {} for e in ENGS}
        self.out_dma = []
        self.n_inst = {e: 0 for e in ENGS}
        self.pe_pending = []

    def _newsem(self, tag):
        self.nsem += 1
        return self.stack.enter_context(self.nc.semaphore(f"s_{tag}_{self.nsem}"))

    def sbuf(self, name, shape, dt):
        return self.stack.enter_context(self.nc.sbuf_tensor(name, list(shape), dt))

    def psum(self, name, shape, dt):
        return self.stack.enter_context(self.nc.psum_tensor(name, list(shape), dt))

    def _need(self, eng, dep, waits):
        if dep is None:
            return
        sem, val = dep[0], dep[1]
        if self.known[eng].get(sem.num, 0) >= val:
            return
        prev = waits.get(sem.num)
        if prev is None or prev[1] < val:
            waits[sem.num] = (sem, val)

    def _collect(self, eng, reads, writes):
        waits = {}
        for b in reads:
            self._need(eng, b.w, waits)
            if b.psum:
                for r in b.rs:
                    if r[2] != eng:
                        self._need(eng, r, waits)
        for b in writes:
            self._need(eng, b.w, waits)
            for r in b.rs:
                self._need(eng, r, waits)
        for sem, val in waits.values():
            self.known[eng][sem.num] = val
            self.ops[eng].append(("wait", sem, val))

    def _mark(self, reads, writes, tok):
        for b in reads:
            b.rs.append(tok)
            if len(b.rs) > 48:
                d = {}
                for r in b.rs:
                    k = (r[0].num, r[2])
                    if k not in d or d[k][1] < r[1]:
                        d[k] = r
                b.rs = list(d.values())
        for b in writes:
            b.w = tok
            b.rs = []

    def op(self, eng, fn, reads=(), writes=(), inc=True):
        self._collect(eng, reads, writes)
        c = self.cur[eng]
        if c[1] >= SEM_ROT:
            c[0] = self._newsem(eng)
            c[1] = 0
        self.n_inst[eng] += 1
        if not inc:
            assert eng == "tensor"
            self.ops[eng].append(("inst", fn, None, 0))
            self.pe_pending.extend(reads)
            return None
        c[1] += 1
        tok = (c[0], c[1], eng)
        self.ops[eng].append(("inst", fn, c[0], 1))
        if eng == "tensor" and self.pe_pending:
            reads = list(reads) + self.pe_pending
            self.pe_pending = []
        self._mark(reads, writes, tok)
        return tok

    def dma(self, eng, fn, reads=(), writes=(), is_output=False):
        self._collect(eng, reads, writes)
        if writes:
            b = writes[0]
            if b.dsem is None:
                b.dsem = [self._newsem("d"), 0]
            sem = b.dsem
        else:
            b = reads[0]
            if b.osem is None:
                b.osem = [self._newsem("o"), 0]
            sem = b.osem
        sem[1] += 16
        tok = (sem[0], sem[1], "dma")
        self.ops[eng].append(("inst", fn, sem[0], 16))
        self.n_inst[eng] += 1
        self._mark(reads, writes, tok)
        if is_output:
            self.out_dma.append(tok)

    def barrier(self, skip=()):
        toks = []
        for e in ENGS:
            c = self.cur[e]
            if c[1] > 0:
                toks.append((c[0], c[1]))
        od = {}
        for sem, val, _ in self.out_dma:
            if sem.num not in od or od[sem.num][1] < val:
                od[sem.num] = (sem, val)
        toks.extend(od.values())
        for e in ENGS:
            if e in skip:
                continue
            for sem, val in toks:
                if self.known[e].get(sem.num, 0) < val:
                    self.known[e][sem.num] = val
                    self.ops[e].append(("wait", sem, val))

    def finish(self):
        waits = {}
        for sem, val, _ in self.out_dma:
            if sem.num not in waits or waits[sem.num][1] < val:
                waits[sem.num] = (sem, val)
        for e in ENGS:
            c = self.cur[e]
            if c[1] > 0:
                waits[c[0].num] = (c[0], c[1])
        for sem, val in waits.values():
            self.ops["sync"].append(("wait", sem, val))

    def replay(self):
        ops = self.ops

        def run(engine, lst):
            for o in lst:
                if o[0] == "wait":
                    engine.wait_ge(o[1], o[2])
                else:
                    ins = o[1](engine)
                    if o[2] is not None:
                        ins.then_inc(o[2], o[3])

        with self.nc.Block() as block:
            @block.tensor
            def _(e):
                run(e, ops["tensor"])

            @block.vector
            def _(e):
                run(e, ops["vector"])

            @block.scalar
            def _(e):
                run(e, ops["scalar"])

            @block.gpsimd
            def _(e):
                run(e, ops["gpsimd"])

            @block.sync
            def _(e):
                run(e, ops["sync"])


def build_program(do_sample=True, nblocks=NB):
    nc = bass.Bass("TRN2", target_bir_lowering=False)

    def din(name, shape):
        return nc.dram_tensor(name, list(shape), F32, kind="ExternalInput").ap()

    def dout(name, shape):
        return nc.dram_tensor(name, list(shape), F32, kind="ExternalOutput").ap()

    x_d = din("x", [SEQ, D]); p_d = din("p", [SEQ, DPLE])
    xs_d = din("xs", [NS, D]); ps_d = din("ps", [NS, DPLE]); s0_d = din("s0", [NS, H, DK, DV])
    g_mix_d = din("g_mix", [D]); w_in_d = din("w_in", [D, DIN]); w_ret_d = din("w_ret_out", [VR, D])
    lnag_d = din("ln_a_g", [DA]); lnab_d = din("ln_a_b", [DA]); ws_d = din("w_s", [GA, 128, 128])
    bs_d = din("b_s", [GA, 128]); w_aout_d = din("w_a_out", [DA, D]); bgate_d = din("b_gate", [2 * D])
    w_o_d = din("w_o", [D, D]); g_ffn_d = din("g_ffn", [D]); w_fg_d = din("w_ff_gate", [D, DFF])
    w_fu_d = din("w_ff_up", [D, DFF]); w_fd_d = din("w_ff_down", [DFF, D]); g_ple_d = din("g_ple", [D])
    w_ple_d = din("w_ple", [DPLE, D]); w_pg_d = din("w_ple_gate", [D, D]); g_fin_d = din("g_final", [D])
    cosT_d = din("c_cosT", [128, SEQ]); sinT_d = din("c_sinT", [128, SEQ])
    cs_s_d = din("c_cos_s", [1, 128]); sn_s_d = din("c_sin_s", [1, 128])
    kdec_d = din("c_kdec", [128, H]); epsT_d = din("c_epsT", [128, H])
    maskT_d = din("c_maskT", [128, H * 128]); tril_d = din("c_tril", [128, 128])

    NPB = 56
    wscr = nc.dram_tensor("wscr", [NPB, 128, 4096], BF16).ap()
    y_d = dout("y", [SEQ, D]); ys_d = dout("ys", [NS, D]); sp_d = dout("sp", [H, DK, DV])
    sso_d = dout("sso", [NS, H, DK, DV]); cv_d = dout("cv", [NS, DA])

    with ExitStack() as st:
        fw = FW(nc, st)

        def V(m, reads, writes, *a, **k):
            return fw.op("vector", I(m, *a, **k), reads, writes)

        def A(m, reads, writes, *a, **k):
            return fw.op("scalar", I(m, *a, **k), reads, writes)

        def G(m, reads, writes, *a, **k):
            return fw.op("gpsimd", I(m, *a, **k), reads, writes)

        def MM(out, lhsT, rhs, start, stop, reads, writes, inc):
            return fw.op("tensor", I("matmul", out, lhsT=lhsT, rhs=rhs, start=start, stop=stop), reads, writes, inc=inc)

        def TR(out, in_, ident_ap, reads, writes, inc):
            return fw.op("tensor", I("transpose", out=out, in_=in_, identity=ident_ap), reads, writes, inc=inc)

        def LD(out, in_, B, q="scalar", xw=(), **k):
            fw.dma(q, I("dma_start", out=out, in_=in_, **k), writes=[B] + list(xw))

        def ST(out, in_, B, q="gpsimd"):
            fw.dma(q, I("dma_start", out=out, in_=in_), reads=[B], is_output=True)

        ident = fw.sbuf("ident", [128, 128], BF16); Bident = Buf("ident")
        i16 = fw.sbuf("i16", [128, 16, 16], BF16); Bi16 = Buf("i16")
        maskT = fw.sbuf("maskT", [128, H * 128], F32); BmaskT = Buf("maskT")
        tril = fw.sbuf("tril", [128, 128], F32); Btril = Buf("tril")
        kdec = fw.sbuf("kdec", [128, H], F32); Bkdec = Buf("kdec")
        epsT = fw.sbuf("epsT", [128, H], F32); BepsT = Buf("epsT")
        wsT = fw.sbuf("wsT", [128, GA, 128], BF16); BwsT = Buf("wsT")
        T2 = fw.sbuf("T2", [128, 16, 128], F32); BT2 = Buf("T2")
        gcols = fw.sbuf("gcols", [128, 3, 8], F32); Bgcols = Buf("gcols")
        lnag = fw.sbuf("lnag", [128, 16], F32); Blnag = Buf("lnag")
        bgate = fw.sbuf("bgate", [128, 16], F32); Bbgate = Buf("bgate")
        gfin = fw.sbuf("gfin", [128, D], F32); Bgfin = Buf("gfin")
        S = fw.sbuf("S", [128, 8, 512], F32); BS = [Buf(f"S{i}") for i in range(8)]
        Sbf = fw.sbuf("Sbf", [128, 8, 512], BF16); BSbf = [Buf(f"Sbf{i}") for i in range(8)]
        ring = [fw.sbuf(f"ring{i}", [128, 4096], BF16) for i in range(NSLOT)]
        Bring = [Buf(f"ring{i}") for i in range(NSLOT)]
        small = fw.sbuf("small", [128, 1024], F32)
        xnT = fw.sbuf("xnT", [128, 8, TB], BF16); BxnT = [Buf(f"xnT{t}") for t in range(NT)]
        yT = fw.sbuf("yT", [128, 16, TB], BF16); ByT = [Buf(f"yT{t}") for t in range(NT)]
        R0 = fw.sbuf("R0", [128, 4096], F32)
        R1 = fw.sbuf("R1", [128, 28672], BF16)
        ytm2 = fw.sbuf("ytm", [128, 2, 2048], BF16); Bytm2 = [Buf("ytm0"), Buf("ytm1")]
        ytm = ytm2[:, 0, :]; Bytm = Bytm2[0]
        onf = fw.sbuf("onf", [128, 4, 512], F32); Bonf = [Buf(f"onf{i}") for i in range(4)]
        xnb = fw.sbuf("xnb", [128, 2, D], BF16); Bxnb = [Buf("xnb0"), Buf("xnb1")]
        sTb = fw.sbuf("sTb", [128, 512], BF16); BsTb = Buf("sTb")
        cst = fw.sbuf("cst", [128, 2, TB], F32); Bcst = Buf("cst")

        _so = [0]

        def sm(n):
            o = _so[0]; _so[0] += n
            assert _so[0] <= 1024 and not (o < 512 < _so[0]), _so[0]
            return small[:, o:o + n]

        def r1(off_b, shape, dt):
            n = int(np.prod(shape[1:]))
            if dt == F32:
                v = R1[:, off_b // 2: off_b // 2 + n * 2].bitcast(F32)
            else:
                v = R1[:, off_b // 2: off_b // 2 + n]
            if len(shape) == 3:
                v = v.rearrange("p (a b) -> p a b", a=shape[1])
            return v

        K = 1024
        qT = r1(0, [128, 8, TB], BF16); BqT = [Buf(f"qT{h}") for h in range(H)]
        kT = r1(8 * K, [128, 8, TB], BF16); BkT = [Buf(f"kT{h}") for h in range(H)]
        k_tm = r1(16 * K, [128, NT, D], BF16); Bk_tm = [Buf(f"ktm{t}") for t in range(NT)]
        v_tm = r1(24 * K, [128, NT, VR], BF16); Bv_tm = [Buf(f"vtm{t}") for t in range(NT)]
        sg = r1(40 * K, [128, NT, VR], BF16); Bsg = [Buf(f"sg{t}") for t in range(NT)]
        uT = r1(0, [128, 16, TB], BF16); BuT = [Buf(f"uT{m}") for m in range(16)]
        n_tm = r1(16 * K, [128, NT, DA], BF16); Bn_tm = [Buf(f"ntm{t}") for t in range(NT)]
        tmpA = r1(40 * K, [128, 2, 512], F32); BtmpA = [Buf("tmpA0"), Buf("tmpA1")]
        tmpB = r1(44 * K, [128, 2, 512], F32); BtmpB = [Buf("tmpB0"), Buf("tmpB1")]
        mergedT = r1(48 * K, [128, 8, TB], BF16); Bmerged = [Buf(f"mg{m}") for m in range(8)]
        gatesT_v = R0[:, 0:4096].bitcast(BF16).rearrange("p (a b) -> p a b", a=16)
        Bgates = [Buf(f"gt{m}") for m in range(16)]
        hx = yT[:].rearrange("p a b -> p (a b)").bitcast(F32).rearrange("p (a b) -> p a b", a=NT); Bhx = [Buf(f"hx{t}") for t in range(NT)]
        hnT = r1(16 * K, [128, 8, TB], BF16); BhnT = [Buf(f"hnT{t}") for t in range(NT)]
        actT = r1(24 * K, [128, 22, TB], BF16); BactT = [Buf(f"act{m}") for m in range(22)]
        pT = r1(46 * K, [128, 2, TB], BF16); BpT = [Buf(f"pT{t}") for t in range(NT)]
        xt = [R0[:, 0:1024], R0[:, 1024:2048]]; Bxt = [Buf("xt0"), Buf("xt1")]
        ropeA = [R0[:, 2048 + i * 512: 2048 + (i + 1) * 512] for i in range(4)]; BropeA = [Buf(f"rA{i}") for i in range(4)]
        ropeB = [R0[:, i * 512:(i + 1) * 512] for i in range(4)]; BropeB = [Buf(f"rB{i}") for i in range(4)]
        sigt = [R0[:, 2048 + i * 512: 2048 + (i + 1) * 512] for i in range(4)]; Bsigt = [Buf(f"sig{i}") for i in range(4)]
        ptile = r1(48 * K, [128, 256], F32); Bptile = Buf("ptile")
        pbf = r1(49 * K, [128, 256], BF16); Bpbf = Buf("pbf")

        NBANK = 6
        banks = [fw.psum(f"bk{i}", [128, 512], F32) for i in range(NBANK)]
        Bbanks = [Buf(f"bk{i}", psum=True) for i in range(NBANK)]
        trb = [fw.psum(f"tr{i}", [128, 1024], BF16) for i in range(2)]
        Btrb = [Buf(f"tr{i}", psum=True) for i in range(2)]
        _bi = [0]; _ti = [0]

        def bank():
            for _ in range(NBANK):
                i = _bi[0] % NBANK; _bi[0] += 1
                if Bbanks[i].free:
                    Bbanks[i].free = False
                    return banks[i], Bbanks[i]
            raise RuntimeError("no free PSUM bank")

        def trbank():
            for _ in range(2):
                i = _ti[0] % 2; _ti[0] += 1
                if Btrb[i].free:
                    Btrb[i].free = False
                    return trb[i], Btrb[i]
            raise RuntimeError("no free tr bank")

        def rel(B):
            B.free = True

        def wview(w, kc0, kc1, n0, n1):
            return w.rearrange("(kc p) n -> p kc n", p=128)[:, kc0:kc1, n0:n1]

        def block_schedule():
            sch = []
            for j in range(2): sch.append(("q", j, wview(w_in_d, 0, 8, j * 512, (j + 1) * 512)))
            for j in range(2): sch.append(("k", j, wview(w_in_d, 0, 8, 1024 + j * 512, 1024 + (j + 1) * 512)))
            for j in range(4): sch.append(("v", j, wview(w_in_d, 0, 8, 2048 + j * 512, 2048 + (j + 1) * 512)))
            for j in range(4): sch.append(("g", j, wview(w_in_d, 0, 8, 4096 + j * 512, 4096 + (j + 1) * 512)))
            for j in range(4): sch.append(("gt", j, wview(w_in_d, 0, 8, 10240 + j * 512, 10240 + (j + 1) * 512)))
            for j in range(4): sch.append(("va", j, wview(w_in_d, 0, 8, 8192 + j * 512, 8192 + (j + 1) * 512)))
            for j in range(4): sch.append(("u", j, wview(w_in_d, 0, 8, 6144 + j * 512, 6144 + (j + 1) * 512)))
            for j in range(4):
                sch.append(("ao", j, wview(w_aout_d, 0, 16, j * 256, (j + 1) * 256)))
                sch.append(("ro", j, wview(w_ret_d, 0, 16, j * 256, (j + 1) * 256)))
            for j in range(2): sch.append(("wo", j, wview(w_o_d, 0, 8, j * 512, (j + 1) * 512)))
            for j in range(6):
                n1 = min(DFF, (j + 1) * 512)
                sch.append(("fg", j, wview(w_fg_d, 0, 8, j * 512, n1)))
                sch.append(("fu", j, wview(w_fu_d, 0, 8, j * 512, n1)))
            for c2 in range(2):
                for kp in range(3):
                    sch.append(("fd", c2 * 3 + kp, wview(w_fd_d, kp * 8, min(22, kp * 8 + 8), c2 * 512, (c2 + 1) * 512)))
            for j in range(2):
                sch.append(("pg", j, wview(w_pg_d, 0, 8, j * 512, (j + 1) * 512)))
                sch.append(("pl", j, wview(w_ple_d, 0, 2, j * 512, (j + 1) * 512)))
            return sch

        npass = nblocks + (1 if do_sample else 0)
        schedule = []
        for _ in range(npass):
            schedule.extend(block_schedule())
        wstate = {"issued": 0, "next": 0}
        assert len(block_schedule()) == NPB
        Bscr = [Buf(f"scr{i // 7}") if i % 7 == 0 else None for i in range(NPB)]
        Bscr = [Bscr[(i // 7) * 7] for i in range(NPB)]

        def writes_back(i):
            if npass < 2:
                return False
            if npass == 2:
                return i < NPB
            return (i < NPB and i % 2 == 0) or (NPB <= i < 2 * NPB and i % 2 == 1)

        def from_fp32(i):
            if npass <= 2:
                return i < NPB
            return i < NPB or (i < 2 * NPB and i % 2 == 1)

        def issue_upto(n):
            while wstate["issued"] < min(n, len(schedule)):
                i = wstate["issued"]
                key, j, src = schedule[i]
                kc, ncol = src.shape[1], src.shape[2]
                slot = i % NSLOT
                if from_fp32(i):
                    dst = ring[slot][:, 0:kc * ncol].rearrange("p (a b) -> p a b", a=kc)
                    fw.dma("gpsimd", I("dma_start", out=dst, in_=src), writes=[Bring[slot]])
                else:
                    fw.dma("sync", I("dma_start", out=ring[slot][:, 0:kc * ncol], in_=wscr[i % NPB, :, 0:kc * ncol]),
                           reads=[Bscr[i % NPB]], writes=[Bring[slot]])
                wstate["issued"] += 1

        def piece(key, j):
            i = wstate["next"]
            k2, j2, src = schedule[i]
            assert (k2, j2) == (key, j), (key, j, k2, j2)
            issue_upto(i + 1 + PF)
            wstate["next"] += 1
            kc, ncol = src.shape[1], src.shape[2]
            slot = i % NSLOT
            if writes_back(i):
                fw.dma("gpsimd", I("dma_start", out=wscr[i % NPB, :, 0:kc * ncol], in_=ring[slot][:, 0:kc * ncol]),
                       reads=[Bring[slot]], writes=[Bscr[i % NPB]])
            return ring[slot][:, 0:kc * ncol].rearrange("p (a b) -> p a b", a=kc), Bring[slot]

        def rms_A(src, Bsrc, P, k):
            ssq = sm(1); rt = sm(1); rs = sm(1)
            Bq = Buf("ssq")
            xb = xnb[0:P, k % 2, :]; Bxb = Bxnb[k % 2]
            A("activation", [Bsrc], [Bxb, Bq], out=xb, in_=src, func=AF.Square, accum_out=ssq[0:P, :])
            A("activation", [Bq], [Bq], out=rt[0:P, :], in_=ssq[0:P, :], func=AF.Sqrt, scale=1.0 / D, bias=EPS)
            V("reciprocal", [Bq], [Bq], out=rs[0:P, :], in_=rt[0:P, :])
            V("tensor_scalar", [Bsrc, Bq], [Bxb], out=xb, in0=src, scalar1=rs[0:P, :], scalar2=None, op0=ALU.mult)
            return xb, Bxb

        def rms_B(xb, Bxb, P, gi, dstT_fn, Bdst, xw=()):
            tb, Btb = trbank()
            tv = tb[:, 0:8 * P].rearrange("p (a b) -> p a b", a=8)
            for kc in range(8):
                TR(tv[:, kc, :], xb[:, kc * 128:(kc + 1) * 128], ident[0:P, 0:P], [Bxb, Bident], [Btb], inc=(kc == 7))
            V("tensor_tensor", [Btb, Bgcols], [Bdst] + list(xw), out=dstT_fn, in0=tv,
              in1=gcols[:, gi, :].unsqueeze(2).broadcast_to([128, 8, P]), op=ALU.mult)
            rel(Btb)

        def rms_pipe(srcs, Bsrcs, gi, dsts, Bdsts, pre=None, xw=()):
            n = len(srcs)
            st_ = [None] * n
            for t in range(n + 1):
                if t < n:
                    if pre is not None:
                        pre(t)
                    st_[t] = rms_A(srcs[t], Bsrcs[t], 128, t)
                if t >= 1:
                    rms_B(st_[t - 1][0], st_[t - 1][1], 128, gi, dsts[t - 1], Bdsts[t - 1], xw=xw)

        def rope(X1, B1, X2, B2, o1, o2, Bo, tmps, Btmps, cosv, sinv, Bcs):
            ta, tb_, tc, td = tmps
            Wa, Wb, Wc, Wd = Btmps
            V("tensor_tensor", [B1, Bcs], Wa, out=ta, in0=X1, in1=cosv, op=ALU.mult)
            V("tensor_tensor", [B2, Bcs], Wb, out=tb_, in0=X2, in1=sinv, op=ALU.mult)
            V("tensor_tensor", [B1, Bcs], Wc, out=tc, in0=X1, in1=sinv, op=ALU.mult)
            V("tensor_tensor", [B2, Bcs], Wd, out=td, in0=X2, in1=cosv, op=ALU.mult)
            G("tensor_tensor", [Wa[0], Wb[0]], [Bo], out=o1, in0=ta, in1=tb_, op=ALU.subtract)
            G("tensor_tensor", [Wc[0], Wd[0]], [Bo], out=o2, in0=tc, in1=td, op=ALU.add)

        def P0_steps(b_):
            tt0 = b_ * TB
            st_ = [None] * NT

            def ld(t):
                LD(xt[t % 2], x_d[tt0 + t * 128: tt0 + (t + 1) * 128, :], Bxt[t % 2], q="sync", xw=Bgates)

            def first():
                LD(cst[:, 0, :], cosT_d[:, tt0:tt0 + TB], Bcst); LD(cst[:, 1, :], sinT_d[:, tt0:tt0 + TB], Bcst)
                ld(0); ld(1)

            def stepA(t):
                st_[t] = rms_A(xt[t % 2], Bxt[t % 2], 128, t)
                if t + 2 < NT:
                    ld(t + 2)

            def stepB(t):
                rms_B(st_[t][0], st_[t][1], 128, 0, xnT[:, :, t * 128:(t + 1) * 128], BxnT[t])

            steps = [first, lambda: stepA(0), lambda: (stepA(1), stepB(0)), lambda: (stepA(2), stepB(1)),
                     lambda: (stepA(3), stepB(2)), lambda: stepB(3)]
            return steps

        def do_P0(b_):
            for f_ in P0_steps(b_):
                f_()

        G("memset", [], [Bident], ident[:], 1.0)
        G("affine_select", [Bident], [Bident], out=ident[:], in_=ident[:], pattern=[[-1, 128]],
          compare_op=ALU.is_equal, fill=0.0, base=0, channel_multiplier=1)
        G("memset", [], [Bi16], i16[:], 1.0)
        G("affine_select", [Bi16], [Bi16], out=i16[:], in_=i16[:], pattern=[[1, 16], [-1, 16]],
          compare_op=ALU.is_equal, fill=0.0, base=0, channel_multiplier=0)
        LD(maskT[:], maskT_d, BmaskT); LD(tril[:], tril_d, Btril)
        LD(kdec[:], kdec_d, Bkdec); LD(epsT[:], epsT_d, BepsT)
        vrows = R1[0:56, 12288:12544].bitcast(F32)
        Bvrows = Buf("vrows")
        identf = R1[0:64, 12544:12672].bitcast(F32)
        Bidentf = Buf("identf")
        G("memset", [], [Bidentf], identf, 1.0)
        G("affine_select", [Bidentf], [Bidentf], out=identf, in_=identf, pattern=[[-1, 64]],
          compare_op=ALU.is_equal, fill=0.0, base=0, channel_multiplier=1)
        r_ = 0
        for gd, nr in ((g_mix_d, 8), (g_ffn_d, 8), (g_ple_d, 8), (lnag_d, 16), (bgate_d, 16)):
            LD(vrows[r_:r_ + nr, :], gd.rearrange("(kc p) -> kc p", p=128), Bvrows)
            r_ += nr
        bkv, Bbkv = bank()
        fw.op("tensor", I("transpose", out=bkv[:, 0:56], in_=vrows, identity=identf[0:56, 0:56]), [Bvrows, Bidentf], [Bbkv], inc=True)
        V("tensor_copy", [Bbkv], [Bgcols], out=gcols[:].rearrange("p a b -> p (a b)"), in_=bkv[:, 0:24])
        V("tensor_copy", [Bbkv], [Blnag], out=lnag[:], in_=bkv[:, 24:40])
        V("tensor_copy", [Bbkv], [Bbgate], out=bgate[:], in_=bkv[:, 40:56])
        rel(Bbkv)
        LD(gfin[:], g_fin_d.unsqueeze(0).partition_broadcast(128), Bgfin)
        issue_upto(1 + PF)
        p0_first = P0_steps(0)
        p0_first[0]()
        wsf = R1[:, 0:2048].bitcast(F32).rearrange("p (g j) -> p g j", g=GA)
        Bwsf = Buf("wsf")
        wsb = R1[:, 2048:3072].rearrange("p (g j) -> p g j", g=GA)
        Bwsb = Buf("wsb")
        LD(wsf, ws_d.rearrange("g i j -> i g j"), Bwsf)
        V("tensor_tensor", [Bwsf, Btril], [Bwsb], out=wsb, in0=wsf, in1=tril[:].unsqueeze(1).broadcast_to([128, GA, 128]), op=ALU.mult)
        tb_, Btb_ = trbank()
        for g in range(GA):
            TR(tb_[:, g * 128:(g + 1) * 128], wsb[:, g, :], ident[:], [Bwsb, Bident], [Btb_], inc=(g == GA - 1))
        V("tensor_copy", [Btb_], [BwsT], out=wsT[:].rearrange("p g i -> p (g i)"), in_=tb_[:, 0:1024])
        rel(Btb_)
        ones_bf = R1[:, 3072:3073]; Bones = Buf("ones")
        G("memset", [], [Bones], ones_bf, 1.0)
        LBt = R1[0:2, 4096:8192].bitcast(F32)
        RBt = R1[0:2, 8192:10240].bitcast(F32)
        BLB = Buf("LB"); BRB = Buf("RB")
        V("memset", [], [BLB], LBt, 1.0)
        LD(LBt[0:1, :], lnab_d.unsqueeze(0), BLB)
        LD(RBt[1:2, :], bs_d.rearrange("g i -> (g i)").unsqueeze(0), BRB)
        for gh in range(2):
            bk, Bb = bank()
            for gg in range(4):
                g = gh * 4 + gg
                MM(bk[0:1, gg * 128:(gg + 1) * 128], ones_bf, wsT[:, g, :], True, True, [Bones, BwsT], [Bb], inc=(gg == 3))
            V("tensor_copy", [Bb], [BRB], out=RBt[0:1, gh * 512:(gh + 1) * 512], in_=bk[0:1, :])
            rel(Bb)
        for q4 in range(4):
            bk, Bb = bank()
            for cq in range(4):
                cc = q4 * 4 + cq
                g = cc // 2
                MM(bk[:, cq * 128:(cq + 1) * 128], LBt[:, cc * 128:(cc + 1) * 128], RBt[:, g * 128:(g + 1) * 128], True, True,
                   [BLB, BRB], [Bb], inc=(cq == 3))
            V("tensor_copy", [Bb], [BT2], out=T2[:, q4 * 4:(q4 + 1) * 4, :].rearrange("p a b -> p (a b)"), in_=bk[:])
            rel(Bb)
        for i in range(8):
            V("memset", [], [BS[i]], S[:, i, :], 0.0)
            G("memset", [], [BSbf[i]], Sbf[:, i, :], 0.0)
        fw.barrier()

        for blk in range(nblocks):
            t0 = blk * TB
            if blk == 0:
                fw.barrier(skip=("sync",))
            _so[0] = (blk % 2) * 512
            if blk == 0:
                for f_ in p0_first[1:]:
                    f_()
            for name, dstT, BdT in (("q", qT, BqT), ("k", kT, BkT)):
                for j in range(2):
                    pc, Bp = piece(name, j)
                    for hh in range(2):
                        h = 2 * j + hh
                        X1, B1 = bank(); X2, B2 = bank()
                        for half, (X, BX) in enumerate(((X1, B1), (X2, B2))):
                            c0 = hh * 256 + half * 128
                            for kc in range(8):
                                MM(X[:], pc[:, kc, c0:c0 + 128], xnT[:, kc, :], kc == 0, kc == 7, [Bp] + BxnT, [BX], inc=(kc == 7))
                        tm, Btm = (ropeA, [[b_] + Bsigt for b_ in BropeA]) if (h % 2 == 0) else (ropeB, [[b_] + Bxt for b_ in BropeB])
                        rope(X1[:], B1, X2[:], B2, dstT[:, 2 * h, :], dstT[:, 2 * h + 1, :], BdT[h], tm, Btm,
                             cst[:, 0, :], cst[:, 1, :], Bcst)
                        rel(B1); rel(B2)
            for t in range(NT):
                tb, Btb = trbank()
                for c in range(8):
                    TR(tb[:, c * 128:(c + 1) * 128], kT[:, c, t * 128:(t + 1) * 128], ident[:], BkT + [Bident], [Btb], inc=(c == 7))
                A("activation", [Btb], [Bk_tm[t]] + BhnT, out=k_tm[:, t, :], in_=tb[:], func=AF.Copy)
                rel(Btb)
            for j in range(4):
                pc, Bp = piece("v", j)
                for t in range(NT):
                    bk, Bb = bank()
                    for kc in range(8):
                        MM(bk[:], xnT[:, kc, t * 128:(t + 1) * 128], pc[:, kc, :], kc == 0, kc == 7, [Bp, BxnT[t]], [Bb], inc=(kc == 7))
                    A("activation", [Bb, Bkdec], [Bv_tm[t]] + BactT, out=v_tm[:, t, j * 512:(j + 1) * 512], in_=bk[:], func=AF.Copy,
                      scale=kdec[:, j:j + 1])
                    rel(Bb)
            for j in range(4):
                pc, Bp = piece("g", j)
                for t in range(NT):
                    bk, Bb = bank()
                    for kc in range(8):
                        MM(bk[:], xnT[:, kc, t * 128:(t + 1) * 128], pc[:, kc, :], kc == 0, kc == 7, [Bp, BxnT[t]], [Bb], inc=(kc == 7))
                    A("activation", [Bb], [Bsg[t]] + BactT + BpT + [Bptile, Bpbf], out=sg[:, t, j * 512:(j + 1) * 512], in_=bk[:], func=AF.Silu)
                    rel(Bb)
            def y_transposes(cq):
                csq = slice(cq * 128, (cq + 1) * 128)
                yb = ytm2[:, cq % 2, :]; Byb = Bytm2[cq % 2]
                for hf in range(2):
                    tb, Btb = trbank()
                    for c8 in range(8):
                        cc = hf * 8 + c8
                        TR(tb[:, c8 * 128:(c8 + 1) * 128], yb[:, cc * 128:(cc + 1) * 128], ident[:], [Byb, Bident], [Btb], inc=(c8 == 7))
                    A("activation", [Btb], [ByT[cq]] + Bhx, out=yT[:, hf * 8:(hf + 1) * 8, csq],
                      in_=tb[:].rearrange("p (a b) -> p a b", a=8), func=AF.Copy)
                    rel(Btb)

            for c in range(NT):
                cs_ = slice(c * 128, (c + 1) * 128)
                scb, Bsc = bank()
                for h in range(H):
                    for half in range(2):
                        MM(scb[:, h * 128:(h + 1) * 128], kT[:, 2 * h + half, cs_], qT[:, 2 * h + half, cs_], half == 0, half == 1,
                           [BkT[h], BqT[h]], [Bsc], inc=(h == H - 1 and half == 1))
                V("tensor_tensor", [Bsc, BmaskT], [BsTb], out=sTb[:], in0=scb[:], in1=maskT[:], op=ALU.mult)
                rel(Bsc)
                st6 = sm(24).rearrange("p (h s) -> p h s", h=H); mv4 = sm(8).rearrange("p (h s) -> p h s", h=H)
                sd4 = sm(4); rs4 = sm(4); nm4 = sm(4)
                Bst = Buf("st")
                for h in range(H):
                    hs = slice(h * 512, (h + 1) * 512)
                    ob, Bo = bank()
                    MM(ob[:], sTb[:, h * 128:(h + 1) * 128], v_tm[:, c, hs], True, False, [BsTb, Bv_tm[c]], [Bo], inc=False)
                    MM(ob[:], qT[:, 2 * h, cs_], Sbf[:, 2 * h, :], False, False, [BqT[h], BSbf[2 * h]], [Bo], inc=False)
                    MM(ob[:], qT[:, 2 * h + 1, cs_], Sbf[:, 2 * h + 1, :], False, True, [BqT[h], BSbf[2 * h + 1]], [Bo], inc=True)
                    A("activation", [Bo], [Bonf[h]], out=onf[:, h, :], in_=ob[:], func=AF.Copy)
                    rel(Bo)
                    for half in range(2):
                        ub, Bu = bank()
                        i8 = 2 * h + half
                        MM(ub[:], k_tm[:, c, h * 256 + half * 128: h * 256 + (half + 1) * 128], v_tm[:, c, hs], True, True,
                           [Bk_tm[c], Bv_tm[c]], [Bu], inc=True)
                        V("scalar_tensor_tensor", [BS[i8], Bu], [BS[i8]], out=S[:, i8, :], in0=S[:, i8, :], scalar=GAML[h],
                          in1=ub[:], op0=ALU.mult, op1=ALU.add)
                        rel(Bu)
                        if half == 0:
                            A("activation", [BS[i8]], [BSbf[i8]], out=Sbf[:, i8, :], in_=S[:, i8, :], func=AF.Copy)
                        else:
                            G("tensor_copy", [BS[i8]], [BSbf[i8]], out=Sbf[:, i8, :], in_=S[:, i8, :])
                    V("bn_stats", [Bonf[h]], [Bst], out=st6[:, h, :], in_=onf[:, h, :])
                pcg, Bpg_ = piece("gt", c)
                for mm in range(4):
                    m = 4 * c + mm
                    bk, Bb = bank()
                    for kc in range(8):
                        MM(bk[:], pcg[:, kc, mm * 128:(mm + 1) * 128], xnT[:, kc, :], kc == 0, kc == 7, [Bpg_] + BxnT, [Bb], inc=(kc == 7))
                    A("activation", [Bb, Bbgate], [Bgates[m]] + BropeA + BropeB + Bxt, out=gatesT_v[:, m, :], in_=bk[:], func=AF.Identity,
                      bias=bgate[:, m:m + 1])
                    rel(Bb)
                for h in range(H):
                    V("bn_aggr", [Bst], [Bst], out=mv4[:, h, :], in_=st6[:, h, :])
                V("tensor_tensor", [Bst, BepsT], [Bst], out=sd4, in0=mv4[:, :, 1], in1=epsT[:], op=ALU.add)
                A("activation", [Bst], [Bst], out=sd4, in_=sd4, func=AF.Sqrt)
                V("reciprocal", [Bst], [Bst], out=rs4, in_=sd4)
                V("scalar_tensor_tensor", [Bst], [Bst], out=nm4, in0=mv4[:, :, 0], scalar=-1.0, in1=rs4, op0=ALU.mult, op1=ALU.mult)
                yb = ytm2[:, c % 2, :]; Byb = Bytm2[c % 2]
                for h in range(H):
                    hs = slice(h * 512, (h + 1) * 512)
                    V("tensor_scalar", [Bonf[h], Bst], [Bonf[h]], out=onf[:, h, :], in0=onf[:, h, :], scalar1=rs4[:, h:h + 1],
                      scalar2=nm4[:, h:h + 1], op0=ALU.mult, op1=ALU.add)
                    G("tensor_tensor", [Bonf[h], Bsg[c]], [Byb], out=yb[:, hs], in0=onf[:, h, :], in1=sg[:, c, hs], op=ALU.mult)
                if c > 0:
                    y_transposes(c - 1)
            if blk == nblocks - 1:
                for i8 in range(8):
                    h, half = i8 // 2, i8 % 2
                    oi = i8 % 2
                    A("activation", [BS[i8]], [Bonf[oi]], out=onf[:, oi, :], in_=S[:, i8, :], func=AF.Copy, scale=1.0 / 16.0)
                    ST(sp_d[h, half * 128:(half + 1) * 128, :], onf[:, oi, :], Bonf[oi])
            stv = sm(NT * 4 * 6); Bstv = [Buf(f"stv{t}") for t in range(NT)]
            stv = stv.rearrange("p (t j s) -> p t j s", t=NT, j=4)
            for j in range(4):
                if j == 1:
                    y_transposes(NT - 1)
                pc, Bp = piece("va", j)
                for t in range(NT):
                    bk, Bb = bank()
                    for kc in range(8):
                        MM(bk[:], xnT[:, kc, t * 128:(t + 1) * 128], pc[:, kc, :], kc == 0, kc == 7, [Bp, BxnT[t]], [Bb], inc=(kc == 7))
                    A("activation", [Bb], [Bn_tm[t]] + Bk_tm + Bv_tm, out=n_tm[:, t, j * 512:(j + 1) * 512], in_=bk[:], func=AF.Gelu)
                    rel(Bb)
                    V("bn_stats", [Bn_tm[t]], [Bstv[t]], out=stv[:, t, j, :], in_=n_tm[:, t, j * 512:(j + 1) * 512])
            for j in range(4):
                pc, Bp = piece("u", j)
                for mm in range(4):
                    m = 4 * j + mm
                    bk, Bb = bank()
                    for kc in range(8):
                        MM(bk[:], pc[:, kc, mm * 128:(mm + 1) * 128], xnT[:, kc, :], kc == 0, kc == 7, [Bp] + BxnT, [Bb], inc=(kc == 7))
                    A("activation", [Bb], [BuT[m]] + BqT + BkT, out=uT[:, m, :], in_=bk[:], func=AF.Gelu)
                    rel(Bb)
            for t in range(NT):
                mv = sm(2); sd = sm(1); rs = sm(1); nmr = sm(1)
                V("bn_aggr", [Bstv[t]], [Bstv[t]], out=mv, in_=stv[:, t, :, :].rearrange("p j s -> p (j s)"))
                A("activation", [Bstv[t]], [Bstv[t]], out=sd, in_=mv[:, 1:2], func=AF.Sqrt, bias=EPS)
                V("reciprocal", [Bstv[t]], [Bstv[t]], out=rs, in_=sd)
                V("scalar_tensor_tensor", [Bstv[t]], [Bstv[t]], out=nmr, in0=mv[:, 0:1], scalar=-1.0, in1=rs, op0=ALU.mult, op1=ALU.mult)
                V("tensor_scalar", [Bn_tm[t], Bstv[t]], [Bn_tm[t]], out=n_tm[:, t, :], in0=n_tm[:, t, :], scalar1=rs, scalar2=nmr,
                  op0=ALU.mult, op1=ALU.add)
            for cc in range(16):
                g = cc // 2
                bk, Bb = bank()
                for tc_ in range(NT):
                    MM(bk[:, tc_ * 128:(tc_ + 1) * 128], n_tm[:, tc_, cc * 128:(cc + 1) * 128], wsT[:, g, :], True, True,
                       [Bn_tm[tc_], BwsT], [Bb], inc=(tc_ == NT - 1))
                ti = cc % 2
                V("scalar_tensor_tensor", [Bb, Blnag, BT2], [BtmpA[ti]] + Bsg, out=tmpA[:, ti, :].rearrange("p (a b) -> p a b", a=NT),
                  in0=bk[:].rearrange("p (a b) -> p a b", a=NT), scalar=lnag[:, cc:cc + 1],
                  in1=T2[:, cc, :].unsqueeze(1).broadcast_to([128, NT, 128]), op0=ALU.mult, op1=ALU.add)
                rel(Bb)
                G("tensor_tensor", [BtmpA[ti], BuT[cc]], [BuT[cc]], out=uT[:, cc, :], in0=tmpA[:, ti, :], in1=uT[:, cc, :], op=ALU.mult)
            for m in range(16):
                mo = (m // 2) + 8 * (m % 2)
                A("activation", [Bgates[mo]], [Bgates[mo]], out=gatesT_v[:, mo, :], in_=gatesT_v[:, mo, :], func=AF.Sigmoid)
            for j in range(4):
                pa, Bpa = piece("ao", j)
                pr, Bpr = piece("ro", j)
                for mm in range(2):
                    m = 2 * j + mm
                    ba, Bba = bank(); bb, Bbb = bank()
                    for kc in range(16):
                        MM(ba[:], pa[:, kc, mm * 128:(mm + 1) * 128], uT[:, kc, :], kc == 0, kc == 15, [Bpa] + BuT, [Bba], inc=(kc == 15))
                    for kc in range(16):
                        MM(bb[:], pr[:, kc, mm * 128:(mm + 1) * 128], yT[:, kc, :], kc == 0, kc == 15, [Bpr] + ByT, [Bbb], inc=(kc == 15))
                    ti = m % 2
                    V("tensor_tensor", [Bba, Bgates[m]], [BtmpA[ti]] + Bsg, out=tmpA[:, ti, :], in0=ba[:], in1=gatesT_v[:, m, :], op=ALU.mult)
                    V("tensor_tensor", [Bbb, Bgates[8 + m]], [BtmpB[ti]] + Bsg, out=tmpB[:, ti, :], in0=bb[:], in1=gatesT_v[:, 8 + m, :], op=ALU.mult)
                    rel(Bba); rel(Bbb)
                    G("tensor_tensor", [BtmpA[ti], BtmpB[ti]], [Bmerged[m]] + Bsg, out=mergedT[:, m, :], in0=tmpA[:, ti, :], in1=tmpB[:, ti, :], op=ALU.add)
            fw.barrier(skip=("sync",))
            for t in range(NT):
                LD(hx[:, t, :], x_d[t0 + t * 128: t0 + (t + 1) * 128, :], Bhx[t], xw=ByT)
            for j in range(2):
                pc, Bp = piece("wo", j)
                for t in range(NT):
                    bk, Bb = bank()
                    for kc in range(8):
                        MM(bk[:], mergedT[:, kc, t * 128:(t + 1) * 128], pc[:, kc, :], kc == 0, kc == 7, [Bp] + Bmerged, [Bb], inc=(kc == 7))
                    V("tensor_tensor", [Bb, Bhx[t]], [Bhx[t]], out=hx[:, t, j * 512:(j + 1) * 512], in0=bk[:], in1=hx[:, t, j * 512:(j + 1) * 512], op=ALU.add)
                    rel(Bb)
            rms_pipe([hx[:, t, :] for t in range(NT)], Bhx, 1, [hnT[:, :, t * 128:(t + 1) * 128] for t in range(NT)], BhnT, xw=Bn_tm)
            p0s = P0_steps(blk + 1) if blk + 1 < nblocks else []
            def p_step(t):
                LD(ptile, p_d[t0 + t * 128: t0 + (t + 1) * 128, :], Bptile, xw=Bmerged)
                V("tensor_copy", [Bptile], [Bpbf] + Bmerged, out=pbf, in_=ptile)
                tb, Btb = trbank()
                for kc in range(2):
                    TR(tb[:, kc * 128:(kc + 1) * 128], pbf[:, kc * 128:(kc + 1) * 128], ident[:], [Bpbf, Bident], [Btb], inc=(kc == 1))
                V("tensor_copy", [Btb], [BpT[t]] + BtmpB, out=pT[:, :, t * 128:(t + 1) * 128], in_=tb[:, 0:256].rearrange("p (a b) -> p a b", a=2))
                rel(Btb)

            for j in range(6):
                if j < len(p0s):
                    p0s[j]()
                if 1 <= j <= NT:
                    p_step(j - 1)
                pg, Bpg = piece("fg", j)
                pu, Bpu = piece("fu", j)
                for mm in range(pg.shape[2] // 128):
                    m = 4 * j + mm
                    bg, Bbg = bank(); bu, Bbu = bank()
                    for kc in range(8):
                        MM(bg[:], pg[:, kc, mm * 128:(mm + 1) * 128], hnT[:, kc, :], kc == 0, kc == 7, [Bpg] + BhnT, [Bbg], inc=(kc == 7))
                    for kc in range(8):
                        MM(bu[:], pu[:, kc, mm * 128:(mm + 1) * 128], hnT[:, kc, :], kc == 0, kc == 7, [Bpu] + BhnT, [Bbu], inc=(kc == 7))
                    si = m % 4
                    A("activation", [Bbg], [Bsigt[si]] + Bgates, out=sigt[si], in_=bg[:], func=AF.Silu)
                    rel(Bbg)
                    V("tensor_tensor", [Bbu, Bsigt[si]], [BactT[m]] + Bn_tm + BtmpA + BtmpB, out=actT[:, m, :], in0=bu[:], in1=sigt[si], op=ALU.mult)
                    rel(Bbu)
            for c2 in range(2):
                db = [bank() for _ in range(NT)]
                for kp in range(3):
                    pc, Bp = piece("fd", c2 * 3 + kp)
                    nk = pc.shape[1]
                    for t in range(NT):
                        for kc in range(nk):
                            m = kp * 8 + kc
                            MM(db[t][0][:], actT[:, m, t * 128:(t + 1) * 128], pc[:, kc, :], (kp == 0 and kc == 0), (kp == 2 and kc == nk - 1),
                               [Bp, BactT[m]], [db[t][1]], inc=(kc == nk - 1))
                for t in range(NT):
                    V("tensor_tensor", [db[t][1], Bhx[t]], [Bhx[t]], out=hx[:, t, c2 * 512:(c2 + 1) * 512], in0=db[t][0][:],
                      in1=hx[:, t, c2 * 512:(c2 + 1) * 512], op=ALU.add)
                    rel(db[t][1])
            rms_pipe([hx[:, t, :] for t in range(NT)], Bhx, 2, [hnT[:, :, t * 128:(t + 1) * 128] for t in range(NT)], BhnT)
            for j in range(2):
                pg, Bpg = piece("pg", j)
                pl, Bpl = piece("pl", j)
                for t in range(NT):
                    bg, Bbg = bank(); bl, Bbl = bank()
                    for kc in range(8):
                        MM(bg[:], hnT[:, kc, t * 128:(t + 1) * 128], pg[:, kc, :], kc == 0, kc == 7, [Bpg, BhnT[t]], [Bbg], inc=(kc == 7))
                    for kc in range(2):
                        MM(bl[:], pT[:, kc, t * 128:(t + 1) * 128], pl[:, kc, :], kc == 0, kc == 1, [Bpl, BpT[t]], [Bbl], inc=(kc == 1))
                    si = t % 4
                    A("activation", [Bbg], [Bsigt[si]], out=sigt[si], in_=bg[:], func=AF.Sigmoid)
                    rel(Bbg)
                    V("tensor_tensor", [Bbl, Bsigt[si]], [Bsigt[si]], out=sigt[si], in0=bl[:], in1=sigt[si], op=ALU.mult)
                    rel(Bbl)
                    G("tensor_tensor", [Bsigt[si], Bhx[t]], [Bhx[t]], out=hx[:, t, j * 512:(j + 1) * 512], in0=sigt[si],
                      in1=hx[:, t, j * 512:(j + 1) * 512], op=ALU.add)
            for t in range(NT):
                ssq = sm(1); rt = sm(1); rs = sm(1); Bq = Buf("fq")
                xb = xnb[:, t % 2, :]; Bxb = Bxnb[t % 2]
                A("activation", [Bhx[t]], [Bxb, Bq], out=xb, in_=hx[:, t, :], func=AF.Square, accum_out=ssq)
                A("activation", [Bq], [Bq], out=rt, in_=ssq, func=AF.Sqrt, scale=1.0 / D, bias=EPS)
                V("reciprocal", [Bq], [Bq], out=rs, in_=rt)
                V("scalar_tensor_tensor", [Bhx[t], Bq, Bgfin], [Bhx[t]], out=hx[:, t, :], in0=hx[:, t, :], scalar=rs, in1=gfin[:],
                  op0=ALU.mult, op1=ALU.mult)
                ST(y_d[t0 + t * 128: t0 + (t + 1) * 128, :], hx[:, t, :], Bhx[t])

        if do_sample:
            fw.barrier()
            _so[0] = 0
            P = NS
            def sv(off, n, dt=F32):
                if dt == F32:
                    return R1[0:P, off // 2: off // 2 + 2 * n].bitcast(F32)
                return R1[0:P, off // 2: off // 2 + n]
            o = 0
            xs_t = sv(o, 1024); o += 4096
            q_s = sv(o, 1024); o += 4096
            k_s = sv(o, 1024); o += 4096
            v_s = sv(o, 2048); o += 8192
            sg_s = sv(o, 2048); o += 8192
            u_s = sv(o, 2048); o += 8192
            gv_s = sv(o, 2048); o += 8192
            gt_s = sv(o, 2048); o += 8192
            vbf_s = sv(o, 2048, BF16); o += 4096
            big_bf = xnb[0:P].rearrange("p a b -> p (a b)")
            Bxs = Buf("xs"); Bq_s = Buf("q_s"); Bk_s = Buf("k_s"); Bv_s = Buf("v_s"); Bsg_s = Buf("sg_s")
            Bu_s = Buf("u_s"); Bgv_s = Buf("gv_s"); Bgt_s = Buf("gt_s"); Bvbf = Buf("vbf"); Bbig = Buf("bigbf")
            yTf = yT[:].rearrange("p a b -> p (a b)")
            xnT_s = yTf[:, 0:128].rearrange("p (a b) -> p a b", a=8); BxnT_s = Buf("xnT_s")
            qT_s = yTf[:, 128:256].rearrange("p (a b) -> p a b", a=8); BqT_s = Buf("qT_s")
            yT_s = yTf[:, 256:512].rearrange("p (a b) -> p a b", a=16); ByT_s = Buf("yT_s")
            usT_s = yTf[:, 512:768].rearrange("p (a b) -> p a b", a=16); BusT_s = Buf("usT_s")
            mgT_s = yTf[:, 768:896].rearrange("p (a b) -> p a b", a=8); BmgT_s = Buf("mgT_s")
            hnT_s = yTf[:, 896:1024].rearrange("p (a b) -> p a b", a=8); BhnT_s = Buf("hnT_s")
            actT_s = yTf[:, 1024:1376].rearrange("p (a b) -> p a b", a=22); BactT_s = Buf("actT_s")
            pT_s = yTf[:, 1376:1408].rearrange("p (a b) -> p a b", a=2); BpT_s = Buf("pT_s")
            qbd = yTf[:, 2048:4096].rearrange("p (c b k) -> p c b k", c=8, b=16); Bqbd = Buf("qbd")
            s0bf = [yTf[:, 4096 + i * 1024: 4096 + (i + 1) * 1024].rearrange("p (a b) -> p a b", a=2) for i in range(2)]
            Bs0bf = [Buf("s0bf0"), Buf("s0bf1")]
            kbd = xnT[0:P].rearrange("p a b -> p (a b)")
            Bkbd = Buf("kbd")
            r0s = R0[0:P, :]
            tA = r0s[:, 0:1024]; tB = r0s[:, 1024:2048]; tC = r0s[:, 2048:3072]; tD = r0s[:, 3072:4096]
            BtA = Buf("tA"); BtB = Buf("tB"); BtC = Buf("tC"); BtD = Buf("tD")
            stt = [S[:, 2 * i:2 * i + 2, :] for i in range(4)]
            stt += [R1[:, (28672 + i * 4096) // 2:(28672 + (i + 1) * 4096) // 2].bitcast(F32).rearrange("p (a b) -> p a b", a=2) for i in range(4)]
            NST = len(stt)
            Bstt = [Buf(f"stt{i}") for i in range(NST)]
            cs_s = cst[0:P, 0, 0:128]; sn_s = cst[0:P, 0, 128:256]; Bcs_s = Buf("cs_s")
            ws00 = cst[0:P, 0, 256:264]; bs0 = cst[0:P, 0, 264:272]; Bws00 = Buf("ws00")
            bcA = Sbf[0:P].rearrange("p a b -> p (a b)").bitcast(F32)
            BbcA = Buf("bcA")
            h_s = onf[0:P, 0:2, :].rearrange("p a b -> p (a b)"); Bh_s = Buf("h_s")
            y_s = ytm2[0:P, 0, :]; By_s = Buf("y_s")

            LD(xs_t, xs_d, Bxs)
            LD(cs_s, cs_s_d.partition_broadcast(P), Bcs_s); LD(sn_s, sn_s_d.partition_broadcast(P), Bcs_s)
            LD(ws00, ws_d[:, 0:1, 0:1].rearrange("g a b -> (a b) g").partition_broadcast(P), Bws00, allow_slow_non_contiguous=True)
            LD(bs0, bs_d[:, 0:1].rearrange("g a -> a g").partition_broadcast(P), Bws00, allow_slow_non_contiguous=True)

            def rms_s(src, Bsrc, gi, dstT, Bdst):
                ssq = sm(1); rt = sm(1); rs = sm(1); Bq = Buf("sq")
                xb = big_bf[:, 0:1024]
                A("activation", [Bsrc], [Bbig, Bq], out=xb, in_=src, func=AF.Square, accum_out=ssq[0:P, :])
                A("activation", [Bq], [Bq], out=rt[0:P, :], in_=ssq[0:P, :], func=AF.Sqrt, scale=1.0 / D, bias=EPS)
                V("reciprocal", [Bq], [Bq], out=rs[0:P, :], in_=rt[0:P, :])
                V("tensor_scalar", [Bsrc, Bq], [Bbig], out=xb, in0=src, scalar1=rs[0:P, :], scalar2=None, op0=ALU.mult)
                tb, Btb = trbank()
                tv = tb[:, 0:8 * P].rearrange("p (a b) -> p a b", a=8)
                for kc in range(8):
                    TR(tv[:, kc, :], xb[:, kc * 128:(kc + 1) * 128], ident[0:P, 0:P], [Bbig, Bident], [Btb], inc=(kc == 7))
                V("tensor_tensor", [Btb, Bgcols], [Bdst], out=dstT, in0=tv, in1=gcols[:, gi, :].unsqueeze(2).broadcast_to([128, 8, P]), op=ALU.mult)
                rel(Btb)
                return rs

            def toT(src_bf, Bsrc, n, dstT, Bdst):
                tb, Btb = trbank()
                tv = tb[:, 0:n * P].rearrange("p (a b) -> p a b", a=n)
                for kc in range(n):
                    TR(tv[:, kc, :], src_bf[:, kc * 128:(kc + 1) * 128], ident[0:P, 0:P], [Bsrc, Bident], [Btb], inc=(kc == n - 1))
                V("tensor_copy", [Btb], [Bdst], out=dstT, in_=tv)
                rel(Btb)

            def proj(pc, Bp, actT_, BaT, nk, ncol):
                bk, Bb = bank()
                for kc in range(nk):
                    MM(bk[0:P, 0:ncol], actT_[:, kc, :], pc[:, kc, :], kc == 0, kc == nk - 1, [Bp, BaT], [Bb], inc=(kc == nk - 1))
                return bk, Bb

            rms_s(xs_t, Bxs, 0, xnT_s, BxnT_s)
            for name, dst, Bd, fn in (("q", q_s, Bq_s, AF.Copy), ("k", k_s, Bk_s, AF.Copy)):
                for j in range(2):
                    pc, Bp = piece(name, j)
                    bk, Bb = proj(pc, Bp, xnT_s, BxnT_s, 8, 512)
                    A("activation", [Bb], [Bd], out=dst[:, j * 512:(j + 1) * 512], in_=bk[0:P, :], func=fn)
                    rel(Bb)
            for name, dst, Bd, fn in (("v", v_s, Bv_s, AF.Copy), ("g", sg_s, Bsg_s, AF.Silu)):
                for j in range(4):
                    pc, Bp = piece(name, j)
                    bk, Bb = proj(pc, Bp, xnT_s, BxnT_s, 8, 512)
                    A("activation", [Bb], [Bd], out=dst[:, j * 512:(j + 1) * 512], in_=bk[0:P, :], func=fn)
                    rel(Bb)
            LD(bcA, bgate_d.unsqueeze(0).partition_broadcast(P), BbcA)
            for j in range(4):
                pc, Bp = piece("gt", j)
                bk, Bb = proj(pc, Bp, xnT_s, BxnT_s, 8, 512)
                V("tensor_tensor", [Bb, BbcA], [Bgt_s], out=gt_s[:, j * 512:(j + 1) * 512], in0=bk[0:P, :], in1=bcA[:, j * 512:(j + 1) * 512], op=ALU.add)
                rel(Bb)
            A("activation", [Bgt_s], [Bgt_s], out=gt_s, in_=gt_s, func=AF.Sigmoid)
            def rope_s(src, Bsrc, scale, dst, Bdst):
                s4 = src.rearrange("p (h t d) -> p h t d", h=H, t=2)
                d4 = dst.rearrange("p (h t d) -> p h t d", h=H, t=2)
                x1 = s4[:, :, 0, :]; x2 = s4[:, :, 1, :]
                cb = cs_s.unsqueeze(1).broadcast_to([P, H, 128]); sb = sn_s.unsqueeze(1).broadcast_to([P, H, 128])
                ta = tA[:, 0:512].rearrange("p (h d) -> p h d", h=H); tb_ = tA[:, 512:1024].rearrange("p (h d) -> p h d", h=H)
                tc = tB[:, 0:512].rearrange("p (h d) -> p h d", h=H); td = tB[:, 512:1024].rearrange("p (h d) -> p h d", h=H)
                V("scalar_tensor_tensor", [Bsrc, Bcs_s], [BtA], out=ta, in0=x1, scalar=scale, in1=cb, op0=ALU.mult, op1=ALU.mult)
                V("scalar_tensor_tensor", [Bsrc, Bcs_s], [BtA], out=tb_, in0=x2, scalar=scale, in1=sb, op0=ALU.mult, op1=ALU.mult)
                V("scalar_tensor_tensor", [Bsrc, Bcs_s], [BtB], out=tc, in0=x1, scalar=scale, in1=sb, op0=ALU.mult, op1=ALU.mult)
                V("scalar_tensor_tensor", [Bsrc, Bcs_s], [BtB], out=td, in0=x2, scalar=scale, in1=cb, op0=ALU.mult, op1=ALU.mult)
                V("tensor_tensor", [BtA, Bsrc], [Bdst], out=d4[:, :, 0, :], in0=ta, in1=tb_, op=ALU.subtract)
                V("tensor_tensor", [BtB, Bsrc], [Bdst], out=d4[:, :, 1, :], in0=tc, in1=td, op=ALU.add)
            rope_s(q_s, Bq_s, 1.0, tC, BtC)
            V("tensor_copy", [BtC], [Bq_s], out=q_s, in_=tC)
            rope_s(k_s, Bk_s, 1.0 / 16.0, tC, BtC)
            V("tensor_copy", [BtC], [Bk_s], out=k_s, in_=tC)
            qk = sm(H); Bqk = Buf("qk")
            V("tensor_tensor", [Bq_s, Bk_s], [BtC], out=tC, in0=q_s, in1=k_s, op=ALU.mult)
            V("tensor_reduce", [BtC], [Bqk], out=qk[0:P, :], in_=tC.rearrange("p (h d) -> p h d", h=H), axis=AX.X, op=ALU.add)
            V("tensor_copy", [Bv_s], [Bvbf], out=vbf_s, in_=v_s)
            V("tensor_copy", [Bq_s], [Bbig], out=big_bf[:, 0:1024], in_=q_s)
            toT(big_bf[:, 0:1024], Bbig, 8, qT_s, BqT_s)
            V("tensor_tensor", [BqT_s, Bi16], [Bqbd], out=qbd, in0=i16[:].unsqueeze(1).broadcast_to([128, 8, 16, 16]),
              in1=qT_s.unsqueeze(3).broadcast_to([128, 8, 16, 16]), op=ALU.mult)
            it = 0
            for h in range(H):
                V("tensor_tensor", [Bk_s, Bident], [Bkbd], out=kbd.rearrange("p (b d) -> p b d", b=16),
                  in0=k_s[:, h * 256:(h + 1) * 256].unsqueeze(1).broadcast_to([P, 16, 256]),
                  in1=ident[0:P, 0:16].unsqueeze(2).broadcast_to([P, 16, 256]), op=ALU.mult)
                ocb, Boc = bank()
                for b in range(NS):
                    si = it % NST; it += 1
                    LD(stt[si], s0_d[b, h].rearrange("(half p) v -> p half v", p=128), Bstt[si], q="sync")
                    ci = it % 2
                    A("activation", [Bstt[si]], [Bs0bf[ci]], out=s0bf[ci], in_=stt[si], func=AF.Copy)
                    for half in range(2):
                        MM(ocb[0:P, :], qbd[:, 2 * h + half, b, :], s0bf[ci][:, half, :], (b == 0 and half == 0), (b == NS - 1 and half == 1),
                           [Bqbd, Bs0bf[ci]], [Boc], inc=(half == 1))
                    for half in range(2):
                        ub, Bu = bank()
                        MM(ub[:], kbd[:, b * 256 + half * 128: b * 256 + (half + 1) * 128], vbf_s[:, h * 512:(h + 1) * 512], True, True,
                           [Bkbd, Bvbf], [Bu], inc=True)
                        V("scalar_tensor_tensor", [Bstt[si], Bu], [Bstt[si]], out=stt[si][:, half, :], in0=stt[si][:, half, :], scalar=GAM[h],
                          in1=ub[:], op0=ALU.mult, op1=ALU.add)
                        rel(Bu)
                    fw.dma("gpsimd", I("dma_start", out=sso_d[b, h].rearrange("(half p) v -> p half v", p=128), in_=stt[si]),
                           reads=[Bstt[si]], is_output=True)
                oh = tD[:, 0:512]
                V("tensor_scalar", [Boc], [BtD], out=oh, in0=ocb[0:P, :], scalar1=GAM[h], scalar2=None, op0=ALU.mult)
                rel(Boc)
                V("scalar_tensor_tensor", [Bv_s, Bqk, BtD], [BtD], out=oh, in0=v_s[:, h * 512:(h + 1) * 512], scalar=qk[0:P, h:h + 1], in1=oh,
                  op0=ALU.mult, op1=ALU.add)
                st6 = sm(6); mv = sm(2); sd = sm(1); rs = sm(1); nmr = sm(1); Bst = Buf("sst")
                V("bn_stats", [BtD], [Bst], out=st6[0:P, :], in_=oh)
                V("bn_aggr", [Bst], [Bst], out=mv[0:P, :], in_=st6[0:P, :])
                A("activation", [Bst], [Bst], out=sd[0:P, :], in_=mv[0:P, 1:2], func=AF.Sqrt, bias=EPS)
                V("reciprocal", [Bst], [Bst], out=rs[0:P, :], in_=sd[0:P, :])
                V("scalar_tensor_tensor", [Bst], [Bst], out=nmr[0:P, :], in0=mv[0:P, 0:1], scalar=-1.0, in1=rs[0:P, :], op0=ALU.mult, op1=ALU.mult)
                V("tensor_scalar", [BtD, Bst], [BtD], out=oh, in0=oh, scalar1=rs[0:P, :], scalar2=nmr[0:P, :], op0=ALU.mult, op1=ALU.add)
                V("tensor_tensor", [BtD, Bsg_s], [By_s], out=y_s[:, h * 512:(h + 1) * 512], in0=oh, in1=sg_s[:, h * 512:(h + 1) * 512], op=ALU.mult)
            toT(y_s, By_s, 16, yT_s, ByT_s)
            fw.barrier()
            for name, dst, Bd, fn in (("va", gv_s, Bgv_s, AF.Gelu), ("u", u_s, Bu_s, AF.Gelu)):
                for j in range(4):
                    pc, Bp = piece(name, j)
                    bk, Bb = proj(pc, Bp, xnT_s, BxnT_s, 8, 512)
                    A("activation", [Bb], [Bd], out=dst[:, j * 512:(j + 1) * 512], in_=bk[0:P, :], func=fn)
                    rel(Bb)
            stq = sm(24); mv = sm(2); sd = sm(1); rs = sm(1); nmr = sm(1); Bsq = Buf("stq")
            for j in range(4):
                V("bn_stats", [Bgv_s], [Bsq], out=stq[0:P, j * 6:(j + 1) * 6], in_=gv_s[:, j * 512:(j + 1) * 512])
            V("bn_aggr", [Bsq], [Bsq], out=mv[0:P, :], in_=stq[0:P, :])
            A("activation", [Bsq], [Bsq], out=sd[0:P, :], in_=mv[0:P, 1:2], func=AF.Sqrt, bias=EPS)
            V("reciprocal", [Bsq], [Bsq], out=rs[0:P, :], in_=sd[0:P, :])
            V("scalar_tensor_tensor", [Bsq], [Bsq], out=nmr[0:P, :], in0=mv[0:P, 0:1], scalar=-1.0, in1=rs[0:P, :], op0=ALU.mult, op1=ALU.mult)
            V("tensor_scalar", [Bgv_s, Bsq], [Bgv_s], out=gv_s, in0=gv_s, scalar1=rs[0:P, :], scalar2=nmr[0:P, :], op0=ALU.mult, op1=ALU.add)
            LD(bcA, lnag_d.unsqueeze(0).partition_broadcast(P), BbcA)
            V("tensor_tensor", [Bgv_s, BbcA], [Bgv_s], out=gv_s, in0=gv_s, in1=bcA, op=ALU.mult)
            LD(bcA, lnab_d.unsqueeze(0).partition_broadcast(P), BbcA)
            V("tensor_tensor", [Bgv_s, BbcA], [Bgv_s], out=gv_s, in0=gv_s, in1=bcA, op=ALU.add)
            ST(cv_d, gv_s, Bgv_s)
            for g in range(GA):
                gs = slice(g * 256, (g + 1) * 256)
                V("tensor_scalar", [Bgv_s, Bws00], [BtC], out=tC[:, 0:256], in0=gv_s[:, gs], scalar1=ws00[:, g:g + 1], scalar2=bs0[:, g:g + 1],
                  op0=ALU.mult, op1=ALU.add)
                V("tensor_tensor", [BtC, Bu_s], [Bbig], out=big_bf[:, gs], in0=tC[:, 0:256], in1=u_s[:, gs], op=ALU.mult)
            toT(big_bf, Bbig, 16, usT_s, BusT_s)
            for j in range(4):
                pa, Bpa = piece("ao", j)
                pr, Bpr = piece("ro", j)
                ba, Bba = proj(pa, Bpa, usT_s, BusT_s, 16, 256)
                bb, Bbb = proj(pr, Bpr, yT_s, ByT_s, 16, 256)
                cs2 = slice(j * 256, (j + 1) * 256)
                V("tensor_tensor", [Bba, Bgt_s], [BtA], out=tA[:, cs2], in0=ba[0:P, 0:256], in1=gt_s[:, cs2], op=ALU.mult)
                V("tensor_tensor", [Bbb, Bgt_s], [BtB], out=tB[:, cs2], in0=bb[0:P, 0:256], in1=gt_s[:, 1024 + j * 256: 1024 + (j + 1) * 256], op=ALU.mult)
                rel(Bba); rel(Bbb)
                V("tensor_tensor", [BtA, BtB], [Bbig], out=big_bf[:, cs2], in0=tA[:, cs2], in1=tB[:, cs2], op=ALU.add)
            toT(big_bf[:, 0:1024], Bbig, 8, mgT_s, BmgT_s)
            for j in range(2):
                pc, Bp = piece("wo", j)
                bk, Bb = proj(pc, Bp, mgT_s, BmgT_s, 8, 512)
                V("tensor_tensor", [Bb, Bxs], [Bh_s], out=h_s[:, j * 512:(j + 1) * 512], in0=bk[0:P, :], in1=xs_t[:, j * 512:(j + 1) * 512], op=ALU.add)
                rel(Bb)
            rms_s(h_s, Bh_s, 1, hnT_s, BhnT_s)
            act_s = u_s.bitcast(BF16)[:, 0:DFF]
            for j in range(6):
                pg, Bpg = piece("fg", j)
                pu, Bpu = piece("fu", j)
                ncol = pg.shape[2]
                bg, Bbg = proj(pg, Bpg, hnT_s, BhnT_s, 8, ncol)
                bu, Bbu = proj(pu, Bpu, hnT_s, BhnT_s, 8, ncol)
                A("activation", [Bbg], [BtC], out=tC[:, 0:ncol], in_=bg[0:P, 0:ncol], func=AF.Silu)
                rel(Bbg)
                V("tensor_tensor", [Bbu, BtC], [Bu_s], out=act_s[:, j * 512: j * 512 + ncol], in0=bu[0:P, 0:ncol], in1=tC[:, 0:ncol], op=ALU.mult)
                rel(Bbu)
            toT(act_s[:, 0:2048], Bu_s, 16, actT_s[:, 0:16, :], BactT_s)
            toT(act_s[:, 2048:DFF], Bu_s, 6, actT_s[:, 16:22, :], BactT_s)
            for c2 in range(2):
                bk, Bb = bank()
                for kp in range(3):
                    pc, Bp = piece("fd", c2 * 3 + kp)
                    nk = pc.shape[1]
                    for kc in range(nk):
                        MM(bk[0:P, :], actT_s[:, kp * 8 + kc, :], pc[:, kc, :], (kp == 0 and kc == 0), (kp == 2 and kc == nk - 1), [Bp, BactT_s], [Bb],
                           inc=(kc == nk - 1))
                V("tensor_tensor", [Bb, Bh_s], [Bh_s], out=h_s[:, c2 * 512:(c2 + 1) * 512], in0=bk[0:P, :], in1=h_s[:, c2 * 512:(c2 + 1) * 512], op=ALU.add)
                rel(Bb)
            ps_t = tD[:, 0:256]
            LD(ps_t, ps_d, BtD)
            V("tensor_copy", [BtD], [Bbig], out=big_bf[:, 0:256], in_=ps_t)
            toT(big_bf[:, 0:256], Bbig, 2, pT_s, BpT_s)
            rms_s(h_s, Bh_s, 2, hnT_s, BhnT_s)
            for j in range(2):
                pg, Bpg = piece("pg", j)
                pl, Bpl = piece("pl", j)
                bg, Bbg = proj(pg, Bpg, hnT_s, BhnT_s, 8, 512)
                bl, Bbl = proj(pl, Bpl, pT_s, BpT_s, 2, 512)
                A("activation", [Bbg], [BtC], out=tC[:, 0:512], in_=bg[0:P, :], func=AF.Sigmoid)
                rel(Bbg)
                V("tensor_tensor", [Bbl, BtC], [BtC], out=tC[:, 0:512], in0=bl[0:P, :], in1=tC[:, 0:512], op=ALU.mult)
                rel(Bbl)
                V("tensor_tensor", [BtC, Bh_s], [Bh_s], out=h_s[:, j * 512:(j + 1) * 512], in0=tC[:, 0:512], in1=h_s[:, j * 512:(j + 1) * 512], op=ALU.add)
            ssq = sm(1); rt = sm(1); rs = sm(1); Bq = Buf("fqs")
            A("activation", [Bh_s], [Bbig, Bq], out=big_bf[:, 0:1024], in_=h_s, func=AF.Square, accum_out=ssq[0:P, :])
            A("activation", [Bq], [Bq], out=rt[0:P, :], in_=ssq[0:P, :], func=AF.Sqrt, scale=1.0 / D, bias=EPS)
            V("reciprocal", [Bq], [Bq], out=rs[0:P, :], in_=rt[0:P, :])
            V("scalar_tensor_tensor", [Bh_s, Bq, Bgfin], [Bh_s], out=h_s, in0=h_s, scalar=rs[0:P, :], in1=gfin[0:P, :], op0=ALU.mult, op1=ALU.mult)
            ST(ys_d, h_s, Bh_s)

        assert wstate["next"] == len(schedule), (wstate, len(schedule))
        fw.finish()
        fw.replay()
        build_program.stats = dict(fw.n_inst)
    return nc


def _constants():
    half = 128
    freqs = (np.float32(10000.0) ** (-np.arange(half, dtype=np.float32) / np.float32(half))).astype(np.float32)
    pos = np.arange(SEQ, dtype=np.float32)
    ang = (pos[None, :] * freqs[:, None]).astype(np.float32)
    cosT = np.cos(ang.astype(np.float64)).astype(np.float32)
    sinT = np.sin(ang.astype(np.float64)).astype(np.float32)
    ang_s = (np.float32(PAST) * freqs).astype(np.float32)
    cos_s = np.cos(ang_s.astype(np.float64)).astype(np.float32)[None, :]
    sin_s = np.sin(ang_s.astype(np.float64)).astype(np.float32)[None, :]
    lg = np.log1p(-np.exp2(-5.0 - np.arange(H, dtype=np.float64)))
    idx = np.arange(128, dtype=np.float64)
    kdec = np.exp(lg[None, :] * (127.0 - idx[:, None])).astype(np.float32)
    c = np.exp(lg[None, :] * (idx[:, None] + 1.0)) / 16.0
    epsT = (EPS / (c * c)).astype(np.float32)
    j = np.arange(128)[:, None]; i = np.arange(128)[None, :]
    causal = (j <= i).astype(np.float64)
    maskT = np.concatenate([causal * np.exp(-lg[h] * 128.0) for h in range(H)], axis=1).astype(np.float32)
    tril = (np.arange(128)[:, None] >= np.arange(128)[None, :]).astype(np.float32)
    return dict(c_cosT=cosT, c_sinT=sinT, c_cos_s=cos_s, c_sin_s=sin_s, c_kdec=kdec, c_epsT=epsT,
                c_maskT=maskT, c_tril=tril)


_NC_CACHE = {}


def kernel(x_prompt, x_sample, state_ret, p_prompt, p_sample, g_mix, w_in, w_ret_out, ln_a_g, ln_a_b,
           w_s, b_s, w_a_out, b_gate, w_o, g_ffn, w_ff_gate, w_ff_up, w_ff_down, g_ple, w_ple,
           w_ple_gate, g_final):
    f = lambda a: np.ascontiguousarray(np.asarray(a, dtype=np.float32))
    if "nc" not in _NC_CACHE:
        _NC_CACHE["nc"] = build_program()
    nc = _NC_CACHE["nc"]
    consts = _constants()
    shared = dict(g_mix=f(g_mix[0]), w_in=f(w_in[0]), w_ret_out=f(w_ret_out[0]), ln_a_g=f(ln_a_g[0]), ln_a_b=f(ln_a_b[0]),
                  w_s=f(w_s[0]), b_s=f(b_s[0]), w_a_out=f(w_a_out[0]), b_gate=f(b_gate[0]), w_o=f(w_o[0]), g_ffn=f(g_ffn[0]),
                  w_ff_gate=f(w_ff_gate[0]), w_ff_up=f(w_ff_up[0]), w_ff_down=f(w_ff_down[0]), g_ple=f(g_ple[0]),
                  w_ple=f(w_ple[0]), w_ple_gate=f(w_ple_gate[0]), g_final=f(g_final))
    shared.update(consts)
    xp = f(x_prompt); pp = f(p_prompt[0]); xsm = f(x_sample)[:, 0, :]; psm = f(p_sample[0])[:, 0, :]; s0 = f(state_ret[0])
    in_maps = []
    for c in range(8):
        m = dict(shared)
        m["x"] = xp[c]; m["p"] = pp[c]
        m["xs"] = np.ascontiguousarray(xsm[c * NS:(c + 1) * NS]); m["ps"] = np.ascontiguousarray(psm[c * NS:(c + 1) * NS])
        m["s0"] = np.ascontiguousarray(s0[c * NS:(c + 1) * NS])
        in_maps.append(m)
    res = run_bass_kernel_spmd(nc, in_maps, core_ids=list(range(8)))
    r = res.results
    y_prompt = np.stack([r[c]["y"] for c in range(8)], axis=0)
    y_sample = np.concatenate([r[c]["ys"] for c in range(8)], axis=0)[:, None, :]
    st_p = np.stack([r[c]["sp"] for c in range(8)], axis=0)[None]
    st_s = np.concatenate([r[c]["sso"] for c in range(8)], axis=0)[None]
    cv = np.concatenate([r[c]["cv"] for c in range(8)], axis=0)[None, :, None, :]
    return (y_prompt.astype(np.float32), y_sample.astype(np.float32), st_p.astype(np.float32),
            st_s.astype(np.float32), cv.astype(np.float32))
```

```python
import numpy as np
from contextlib import ExitStack
import concourse.bass as bass
import concourse.mybir as mybir
from concourse.bass_utils import run_bass_kernel_spmd

F32 = mybir.dt.float32
BF16 = mybir.dt.bfloat16
AF = mybir.ActivationFunctionType
ALU = mybir.AluOpType
AX = mybir.AxisListType

D = 1024
SEQ = 2048
NS = 16
H = 4
DK = 256
DV = 512
VR = 2048
DA = 2048
GA = 8
DFF = 2816
DPLE = 256
DIN = 12288
EPS = 1e-6
PAST = 16384
TB = 512
NT = TB // 128
NB = SEQ // TB
NSLOT = 5
PF = 3
GAM = [float(1.0 - 2.0 ** (-5.0 - h)) for h in range(H)]
GAML = [float(np.float32(g) ** 128) for g in GAM]

ENGS = ("tensor", "vector", "scalar", "gpsimd", "sync")
SEM_ROT = 30000


class Buf:
    __slots__ = ("name", "w", "rs", "psum", "dsem", "osem", "free")

    def __init__(self, name, psum=False):
        self.name = name
        self.w = None
        self.rs = []
        self.psum = psum
        self.dsem = None
        self.osem = None
        self.free = True


def I(method, *args, **kw):
    return lambda e: getattr(e, method)(*args, **kw)


class FW:
    def __init__(self, nc, stack):
        self.nc = nc
        self.stack = stack
        self.ops = {e: [] for e in ENGS}
        self.cur = {}
        self.nsem = 0
        for e in ENGS:
            self.cur[e] = [self._newsem(e), 0]
        self.known = {e: {} for e in ENGS}
        self.out_dma = []
        self.n_inst = {e: 0 for e in ENGS}
        self.pe_pending = []

    def _newsem(self, tag):
        self.nsem += 1
        return self.stack.enter_context(self.nc.semaphore(f"s_{tag}_{self.nsem}"))

    def sbuf(self, name, shape, dt):
        return self.stack.enter_context(self.nc.sbuf_tensor(name, list(shape), dt))

    def psum(self, name, shape, dt):
        return self.stack.enter_context(self.nc.psum_tensor(name, list(shape), dt))

    def _need(self, eng, dep, waits):
        if dep is None:
            return
        sem, val = dep[0], dep[1]
        if self.known[eng].get(sem.num, 0) >= val:
            return
        prev = waits.get(sem.num)
        if prev is None or prev[1] < val:
            waits[sem.num] = (sem, val)

    def _collect(self, eng, reads, writes):
        waits = {}
        for b in reads:
            self._need(eng, b.w, waits)
            if b.psum:
                for r in b.rs:
                    if r[2] != eng:
                        self._need(eng, r, waits)
        for b in writes:
            self._need(eng, b.w, waits)
            for r in b.rs:
                self._need(eng, r, waits)
        for sem, val in waits.values():
            self.known[eng][sem.num] = val
            self.ops[eng].append(("wait", sem, val))

    def _mark(self, reads, writes, tok):
        for b in reads:
            b.rs.append(tok)
            if len(b.rs) > 48:
                d = {}
                for r in b.rs:
                    k = (r[0].num, r[2])
                    if k not in d or d[k][1] < r[1]:
                        d[k] = r
                b.rs = list(d.values())
        for b in writes:
            b.w = tok
            b.rs = []

    def op(self, eng, fn, reads=(), writes=(), inc=True):
        self._collect(eng, reads, writes)
        c = self.cur[eng]
        if c[1] >= SEM_ROT:
            c[0] = self._newsem(eng)
            c[1] = 0
        self.n_inst[eng] += 1
        if not inc:
            assert eng == "tensor"
            self.ops[eng].append(("inst", fn, None, 0))
            self.pe_pending.extend(reads)
            return None
        c[1] += 1
        tok = (c[0], c[1], eng)
        self.ops[eng].append(("inst", fn, c[0], 1))
        if eng == "tensor" and self.pe_pending:
            reads = list(reads) + self.pe_pending
            self.pe_pending = []
        self._mark(reads, writes, tok)
        return tok

    def dma(self, eng, fn, reads=(), writes=(), is_output=False):
        self._collect(eng, reads, writes)
        if writes:
            b = writes[0]
            if b.dsem is None:
                b.dsem = [self._newsem("d"), 0]
            sem = b.dsem
        else:
            b = reads[0]
            if b.osem is None:
                b.osem = [self._newsem("o"), 0]
            sem = b.osem
        sem[1] += 16
        tok = (sem[0], sem[1], "dma")
        self.ops[eng].append(("inst", fn, sem[0], 16))
        self.n_inst[eng] += 1
        self._mark(reads, writes, tok)
        if is_output:
            self.out_dma.append(tok)

    def barrier(self, skip=()):
        toks = []
        for e in ENGS:
            c = self.cur[e]
            if c[1] > 0:
                toks.append((c[0], c[1]))
        od = {}
        for sem, val, _ in self.out_dma:
            if sem.num not in od or od[sem.num][1] < val:
                od[sem.num] = (sem, val)
        toks.extend(od.values())
        for e in ENGS:
            if e in skip:
                continue
            for sem, val in toks:
                if self.known[e].get(sem.num, 0) < val:
                    self.known[e][sem.num] = val
                    self.ops[e].append(("wait", sem, val))

    def finish(self):
        waits = {}
        for sem, val, _ in self.out_dma:
            if sem.num not in waits or waits[sem.num][1] < val:
                waits[sem.num] = (sem, val)
        for e in ENGS:
            c = self.cur[e]
            if c[1] > 0:
                waits[c[0].num] = (c[0], c[1])
        for sem, val in waits.values():
            self.ops["sync"].append(("wait", sem, val))

    def replay(self):
        ops = self.ops

        def run(engine, lst):
            for o in lst:
                if o[0] == "wait":
                    engine.wait_ge(o[1], o[2])
                else:
                    ins = o[1](engine)
                    if o[2] is not None:
                        ins.then_inc(o[2], o[3])

        with self.nc.Block() as block:
            @block.tensor
            def _(e):
                run(e, ops["tensor"])

            @block.vector
            def _(e):
                run(e, ops["vector"])

            @block.scalar
            def _(e):
                run(e, ops["scalar"])

            @block.gpsimd
            def _(e):
                run(e, ops["gpsimd"])

            @block.sync
            def _(e):
                run(e, ops["sync"])


def build_program(do_sample=True, nblocks=NB):
    nc = bass.Bass("TRN2", target_bir_lowering=False)

    def din(name, shape):
        return nc.dram_tensor(name, list(shape), F32, kind="ExternalInput").ap()

    def dout(name, shape):
        return nc.dram_tensor(name, list(shape), F32, kind="ExternalOutput").ap()

    x_d = din("x", [SEQ, D]); p_d = din("p", [SEQ, DPLE])
    xs_d = din("xs", [NS, D]); ps_d = din("ps", [NS, DPLE]); s0_d = din("s0", [NS, H, DK, DV])
    g_mix_d = din("g_mix", [D]); w_in_d = din("w_in", [D, DIN]); w_ret_d = din("w_ret_out", [VR, D])
    lnag_d = din("ln_a_g", [DA]); lnab_d = din("ln_a_b", [DA]); ws_d = din("w_s", [GA, 128, 128])
    bs_d = din("b_s", [GA, 128]); w_aout_d = din("w_a_out", [DA, D]); bgate_d = din("b_gate", [2 * D])
    w_o_d = din("w_o", [D, D]); g_ffn_d = din("g_ffn", [D]); w_fg_d = din("w_ff_gate", [D, DFF])
    w_fu_d = din("w_ff_up", [D, DFF]); w_fd_d = din("w_ff_down", [DFF, D]); g_ple_d = din("g_ple", [D])
    w_ple_d = din("w_ple", [DPLE, D]); w_pg_d = din("w_ple_gate", [D, D]); g_fin_d = din("g_final", [D])
    cosT_d = din("c_cosT", [128, SEQ]); sinT_d = din("c_sinT", [128, SEQ])
    cs_s_d = din("c_cos_s", [1, 128]); sn_s_d = din("c_sin_s", [1, 128])
    kdec_d = din("c_kdec", [128, H]); epsT_d = din("c_epsT", [128, H])
    maskT_d = din("c_maskT", [128, H * 128]); tril_d = din("c_tril", [128, 128])

    NPB = 56
    wscr = nc.dram_tensor("wscr", [NPB, 128, 4096], BF16).ap()
    y_d = dout("y", [SEQ, D]); ys_d = dout("ys", [NS, D]); sp_d = dout("sp", [H, DK, DV])
    sso_d = dout("sso", [NS, H, DK, DV]); cv_d = dout("cv", [NS, DA])

    with ExitStack() as st:
        fw = FW(nc, st)

        def V(m, reads, writes, *a, **k):
            return fw.op("vector", I(m, *a, **k), reads, writes)

        def A(m, reads, writes, *a, **k):
            return fw.op("scalar", I(m, *a, **k), reads, writes)

        def G(m, reads, writes, *a, **k):
            return fw.op("gpsimd", I(m, *a, **k), reads, writes)

        def MM(out, lhsT, rhs, start, stop, reads, writes, inc):
            return fw.op("tensor", I("matmul", out, lhsT=lhsT, rhs=rhs, start=start, stop=stop), reads, writes, inc=inc)

        def TR(out, in_, ident_ap, reads, writes, inc):
            return fw.op("tensor", I("transpose", out=out, in_=in_, identity=ident_ap), reads, writes, inc=inc)

        def LD(out, in_, B, q="scalar", xw=(), **k):
            fw.dma(q, I("dma_start", out=out, in_=in_, **k), writes=[B] + list(xw))

        def ST(out, in_, B, q="gpsimd"):
            fw.dma(q, I("dma_start", out=out, in_=in_), reads=[B], is_output=True)

        ident = fw.sbuf("ident", [128, 128], BF16); Bident = Buf("ident")
        i16 = fw.sbuf("i16", [128, 16, 16], BF16); Bi16 = Buf("i16")
        maskT = fw.sbuf("maskT", [128, H * 128], F32); BmaskT = Buf("maskT")
        tril = fw.sbuf("tril", [128, 128], F32); Btril = Buf("tril")
        kdec = fw.sbuf("kdec", [128, H], F32); Bkdec = Buf("kdec")
        epsT = fw.sbuf("epsT", [128, H], F32); BepsT = Buf("epsT")
        wsT = fw.sbuf("wsT", [128, GA, 128], BF16); BwsT = Buf("wsT")
        T2 = fw.sbuf("T2", [128, 16, 128], F32); BT2 = Buf("T2")
        gcols = fw.sbuf("gcols", [128, 3, 8], F32); Bgcols = Buf("gcols")
        lnag = fw.sbuf("lnag", [128, 16], F32); Blnag = Buf("lnag")
        bgate = fw.sbuf("bgate", [128, 16], F32); Bbgate = Buf("bgate")
        gfin = fw.sbuf("gfin", [128, D], F32); Bgfin = Buf("gfin")
        S = fw.sbuf("S", [128, 8, 512], F32); BS = [Buf(f"S{i}") for i in range(8)]
        Sbf = fw.sbuf("Sbf", [128, 8, 512], BF16); BSbf = [Buf(f"Sbf{i}") for i in range(8)]
        ring = [fw.sbuf(f"ring{i}", [128, 4096], BF16) for i in range(NSLOT)]
        Bring = [Buf(f"ring{i}") for i in range(NSLOT)]
        small = fw.sbuf("small", [128, 1024], F32)
        xnT = fw.sbuf("xnT", [128, 8, TB], BF16); BxnT = [Buf(f"xnT{t}") for t in range(NT)]
        yT = fw.sbuf("yT", [128, 16, TB], BF16); ByT = [Buf(f"yT{t}") for t in range(NT)]
        R0 = fw.sbuf("R0", [128, 4096], F32)
        R1 = fw.sbuf("R1", [128, 28672], BF16)
        ytm2 = fw.sbuf("ytm", [128, 2, 2048], BF16); Bytm2 = [Buf("ytm0"), Buf("ytm1")]
        ytm = ytm2[:, 0, :]; Bytm = Bytm2[0]
        onf = fw.sbuf("onf", [128, 4, 512], F32); Bonf = [Buf(f"onf{i}") for i in range(4)]
        xnb = fw.sbuf("xnb", [128, 2, D], BF16); Bxnb = [Buf("xnb0"), Buf("xnb1")]
        sTb = fw.sbuf("sTb", [128, 512], BF16); BsTb = Buf("sTb")
        cst = fw.sbuf("cst", [128, 2, TB], F32); Bcst = Buf("cst")

        _so = [0]

        def sm(n):
            o = _so[0]; _so[0] += n
            assert _so[0] <= 1024 and not (o < 512 < _so[0]), _so[0]
            return small[:, o:o + n]

        def r1(off_b, shape, dt):
            n = int(np.prod(shape[1:]))
            if dt == F32:
                v = R1[:, off_b // 2: off_b // 2 + n * 2].bitcast(F32)
            else:
                v = R1[:, off_b // 2: off_b // 2 + n]
            if len(shape) == 3:
                v = v.rearrange("p (a b) -> p a b", a=shape[1])
            return v

        K = 1024
        qT = r1(0, [128, 8, TB], BF16); BqT = [Buf(f"qT{h}") for h in range(H)]
        kT = r1(8 * K, [128, 8, TB], BF16); BkT = [Buf(f"kT{h}") for h in range(H)]
        k_tm = r1(16 * K, [128, NT, D], BF16); Bk_tm = [Buf(f"ktm{t}") for t in range(NT)]
        v_tm = r1(24 * K, [128, NT, VR], BF16); Bv_tm = [Buf(f"vtm{t}") for t in range(NT)]
        sg = r1(40 * K, [128, NT, VR], BF16); Bsg = [Buf(f"sg{t}") for t in range(NT)]
        uT = r1(0, [128, 16, TB], BF16); BuT = [Buf(f"uT{m}") for m in range(16)]
        n_tm = r1(16 * K, [128, NT, DA], BF16); Bn_tm = [Buf(f"ntm{t}") for t in range(NT)]
        tmpA = r1(40 * K, [128, 2, 512], F32); BtmpA = [Buf("tmpA0"), Buf("tmpA1")]
        tmpB = r1(44 * K, [128, 2, 512], F32); BtmpB = [Buf("tmpB0"), Buf("tmpB1")]
        mergedT = r1(48 * K, [128, 8, TB], BF16); Bmerged = [Buf(f"mg{m}") for m in range(8)]
        gatesT_v = R0[:, 0:4096].bitcast(BF16).rearrange("p (a b) -> p a b", a=16)
        Bgates = [Buf(f"gt{m}") for m in range(16)]
        hx = yT[:].rearrange("p a b -> p (a b)").bitcast(F32).rearrange("p (a b) -> p a b", a=NT); Bhx = [Buf(f"hx{t}") for t in range(NT)]
        hnT = r1(16 * K, [128, 8, TB], BF16); BhnT = [Buf(f"hnT{t}") for t in range(NT)]
        actT = r1(24 * K, [128, 22, TB], BF16); BactT = [Buf(f"act{m}") for m in range(22)]
        pT = r1(46 * K, [128, 2, TB], BF16); BpT = [Buf(f"pT{t}") for t in range(NT)]
        xt = [R0[:, 0:1024], R0[:, 1024:2048]]; Bxt = [Buf("xt0"), Buf("xt1")]
        ropeA = [R0[:, 2048 + i * 512: 2048 + (i + 1) * 512] for i in range(4)]; BropeA = [Buf(f"rA{i}") for i in range(4)]
        ropeB = [R0[:, i * 512:(i + 1) * 512] for i in range(4)]; BropeB = [Buf(f"rB{i}") for i in range(4)]
        sigt = [R0[:, 2048 + i * 512: 2048 + (i + 1) * 512] for i in range(4)]; Bsigt = [Buf(f"sig{i}") for i in range(4)]
        ptile = r1(48 * K, [128, 256], F32); Bptile = Buf("ptile")
        pbf = r1(49 * K, [128, 256], BF16); Bpbf = Buf("pbf")

        NBANK = 6
        banks = [fw.psum(f"bk{i}", [128, 512], F32) for i in range(NBANK)]
        Bbanks = [Buf(f"bk{i}", psum=True) for i in range(NBANK)]
        trb = [fw.psum(f"tr{i}", [128, 1024], BF16) for i in range(2)]
        Btrb = [Buf(f"tr{i}", psum=True) for i in range(2)]
        _bi = [0]; _ti = [0]

        def bank():
            for _ in range(NBANK):
                i = _bi[0] % NBANK; _bi[0] += 1
                if Bbanks[i].free:
                    Bbanks[i].free = False
                    return banks[i], Bbanks[i]
            raise RuntimeError("no free PSUM bank")

        def trbank():
            for _ in range(2):
                i = _ti[0] % 2; _ti[0] += 1
                if Btrb[i].free:
                    Btrb[i].free = False
                    return trb[i], Btrb[i]
            raise RuntimeError("no free tr bank")

        def rel(B):
            B.free = True

        def wview(w, kc0, kc1, n0, n1):
            return w.rearrange("(kc p) n -> p kc n", p=128)[:, kc0:kc1, n0:n1]

        def block_schedule():
            sch = []
            for j in range(2): sch.append(("q", j, wview(w_in_d, 0, 8, j * 512, (j + 1) * 512)))
            for j in range(2): sch.append(("k", j, wview(w_in_d, 0, 8, 1024 + j * 512, 1024 + (j + 1) * 512)))
            for j in range(4): sch.append(("v", j, wview(w_in_d, 0, 8, 2048 + j * 512, 2048 + (j + 1) * 512)))
            for j in range(4): sch.append(("g", j, wview(w_in_d, 0, 8, 4096 + j * 512, 4096 + (j + 1) * 512)))
            for j in range(4): sch.append(("gt", j, wview(w_in_d, 0, 8, 10240 + j * 512, 10240 + (j + 1) * 512)))
            for j in range(4): sch.append(("va", j, wview(w_in_d, 0, 8, 8192 + j * 512, 8192 + (j + 1) * 512)))
            for j in range(4): sch.append(("u", j, wview(w_in_d, 0, 8, 6144 + j * 512, 6144 + (j + 1) * 512)))
            for j in range(4):
                sch.append(("ao", j, wview(w_aout_d, 0, 16, j * 256, (j + 1) * 256)))
                sch.append(("ro", j, wview(w_ret_d, 0, 16, j * 256, (j + 1) * 256)))
            for j in range(2): sch.append(("wo", j, wview(w_o_d, 0, 8, j * 512, (j + 1) * 512)))
            for j in range(6):
                n1 = min(DFF, (j + 1) * 512)
                sch.append(("fg", j, wview(w_fg_d, 0, 8, j * 512, n1)))
                sch.append(("fu", j, wview(w_fu_d, 0, 8, j * 512, n1)))
            for c2 in range(2):
                for kp in range(3):
                    sch.append(("fd", c2 * 3 + kp, wview(w_fd_d, kp * 8, min(22, kp * 8 + 8), c2 * 512, (c2 + 1) * 512)))
            for j in range(2):
                sch.append(("pg", j, wview(w_pg_d, 0, 8, j * 512, (j + 1) * 512)))
                sch.append(("pl", j, wview(w_ple_d, 0, 2, j * 512, (j + 1) * 512)))
            return sch

        npass = nblocks + (1 if do_sample else 0)
        schedule = []
        for _ in range(npass):
            schedule.extend(block_schedule())
        wstate = {"issued": 0, "next": 0}
        assert len(block_schedule()) == NPB
        Bscr = [Buf(f"scr{i // 7}") if i % 7 == 0 else None for i in range(NPB)]
        Bscr = [Bscr[(i // 7) * 7] for i in range(NPB)]

        def writes_back(i):
            if npass < 2:
                return False
            if npass == 2:
                return i < NPB
            return (i < NPB and i % 2 == 0) or (NPB <= i < 2 * NPB and i % 2 == 1)

        def from_fp32(i):
            if npass <= 2:
                return i < NPB
            return i < NPB or (i < 2 * NPB and i % 2 == 1)

        def issue_upto(n):
            while wstate["issued"] < min(n, len(schedule)):
                i = wstate["issued"]
                key, j, src = schedule[i]
                kc, ncol = src.shape[1], src.shape[2]
                slot = i % NSLOT
                if from_fp32(i):
                    dst = ring[slot][:, 0:kc * ncol].rearrange("p (a b) -> p a b", a=kc)
                    fw.dma("gpsimd", I("dma_start", out=dst, in_=src), writes=[Bring[slot]])
                else:
                    fw.dma("sync", I("dma_start", out=ring[slot][:, 0:kc * ncol], in_=wscr[i % NPB, :, 0:kc * ncol]),
                           reads=[Bscr[i % NPB]], writes=[Bring[slot]])
                wstate["issued"] += 1

        def piece(key, j):
            i = wstate["next"]
            k2, j2, src = schedule[i]
            assert (k2, j2) == (key, j), (key, j, k2, j2)
            issue_upto(i + 1 + PF)
            wstate["next"] += 1
            kc, ncol = src.shape[1], src.shape[2]
            slot = i % NSLOT
            if writes_back(i):
                fw.dma("gpsimd", I("dma_start", out=wscr[i % NPB, :, 0:kc * ncol], in_=ring[slot][:, 0:kc * ncol]),
                       reads=[Bring[slot]], writes=[Bscr[i % NPB]])
            return ring[slot][:, 0:kc * ncol].rearrange("p (a b) -> p a b", a=kc), Bring[slot]

        def rms_A(src, Bsrc, P, k):
            ssq = sm(1); rt = sm(1); rs = sm(1)
            Bq = Buf("ssq")
            xb = xnb[0:P, k % 2, :]; Bxb = Bxnb[k % 2]
            A("activation", [Bsrc], [Bxb, Bq], out=xb, in_=src, func=AF.Square, accum_out=ssq[0:P, :])
            A("activation", [Bq], [Bq], out=rt[0:P, :], in_=ssq[0:P, :], func=AF.Sqrt, scale=1.0 / D, bias=EPS)
            V("reciprocal", [Bq], [Bq], out=rs[0:P, :], in_=rt[0:P, :])
            V("tensor_scalar", [Bsrc, Bq], [Bxb], out=xb, in0=src, scalar1=rs[0:P, :], scalar2=None, op0=ALU.mult)
            return xb, Bxb

        def rms_B(xb, Bxb, P, gi, dstT_fn, Bdst):
            tb, Btb = trbank()
            tv = tb[:, 0:8 * P].rearrange("p (a b) -> p a b", a=8)
            for kc in range(8):
                TR(tv[:, kc, :], xb[:, kc * 128:(kc + 1) * 128], ident[0:P, 0:P], [Bxb, Bident], [Btb], inc=(kc == 7))
            V("tensor_tensor", [Btb, Bgcols], [Bdst], out=dstT_fn, in0=tv,
              in1=gcols[:, gi, :].unsqueeze(2).broadcast_to([128, 8, P]), op=ALU.mult)
            rel(Btb)

        def rms_pipe(srcs, Bsrcs, gi, dsts, Bdsts, pre=None):
            n = len(srcs)
            st_ = [None] * n
            for t in range(n + 1):
                if t < n:
                    if pre is not None:
                        pre(t)
                    st_[t] = rms_A(srcs[t], Bsrcs[t], 128, t)
                if t >= 1:
                    rms_B(st_[t - 1][0], st_[t - 1][1], 128, gi, dsts[t - 1], Bdsts[t - 1])

        def rope(X1, B1, X2, B2, o1, o2, Bo, tmps, Btmps, cosv, sinv, Bcs):
            ta, tb_, tc, td = tmps
            Wa, Wb, Wc, Wd = Btmps
            V("tensor_tensor", [B1, Bcs], Wa, out=ta, in0=X1, in1=cosv, op=ALU.mult)
            V("tensor_tensor", [B2, Bcs], Wb, out=tb_, in0=X2, in1=sinv, op=ALU.mult)
            V("tensor_tensor", [B1, Bcs], Wc, out=tc, in0=X1, in1=sinv, op=ALU.mult)
            V("tensor_tensor", [B2, Bcs], Wd, out=td, in0=X2, in1=cosv, op=ALU.mult)
            G("tensor_tensor", [Wa[0], Wb[0]], [Bo], out=o1, in0=ta, in1=tb_, op=ALU.subtract)
            G("tensor_tensor", [Wc[0], Wd[0]], [Bo], out=o2, in0=tc, in1=td, op=ALU.add)

        def P0_steps(b_):
            tt0 = b_ * TB
            st_ = [None] * NT

            def ld(t):
                LD(xt[t % 2], x_d[tt0 + t * 128: tt0 + (t + 1) * 128, :], Bxt[t % 2], q="sync", xw=Bgates)

            def first():
                LD(cst[:, 0, :], cosT_d[:, tt0:tt0 + TB], Bcst); LD(cst[:, 1, :], sinT_d[:, tt0:tt0 + TB], Bcst)
                ld(0); ld(1)

            def stepA(t):
                st_[t] = rms_A(xt[t % 2], Bxt[t % 2], 128, t)
                if t + 2 < NT:
                    ld(t + 2)

            def stepB(t):
                rms_B(st_[t][0], st_[t][1], 128, 0, xnT[:, :, t * 128:(t + 1) * 128], BxnT[t])

            steps = [first, lambda: stepA(0), lambda: (stepA(1), stepB(0)), lambda: (stepA(2), stepB(1)),
                     lambda: (stepA(3), stepB(2)), lambda: stepB(3)]
            return steps

        def do_P0(b_):
            for f_ in P0_steps(b_):
                f_()

        G("memset", [], [Bident], ident[:], 1.0)
        G("affine_select", [Bident], [Bident], out=ident[:], in_=ident[:], pattern=[[-1, 128]],
          compare_op=ALU.is_equal, fill=0.0, base=0, channel_multiplier=1)
        G("memset", [], [Bi16], i16[:], 1.0)
        G("affine_select", [Bi16], [Bi16], out=i16[:], in_=i16[:], pattern=[[1, 16], [-1, 16]],
          compare_op=ALU.is_equal, fill=0.0, base=0, channel_multiplier=0)
        LD(maskT[:], maskT_d, BmaskT); LD(tril[:], tril_d, Btril)
        LD(kdec[:], kdec_d, Bkdec); LD(epsT[:], epsT_d, BepsT)
        vrows = R1[0:56, 12288:12544].bitcast(F32)
        Bvrows = Buf("vrows")
        identf = R1[0:64, 12544:12672].bitcast(F32)
        Bidentf = Buf("identf")
        G("memset", [], [Bidentf], identf, 1.0)
        G("affine_select", [Bidentf], [Bidentf], out=identf, in_=identf, pattern=[[-1, 64]],
          compare_op=ALU.is_equal, fill=0.0, base=0, channel_multiplier=1)
        r_ = 0
        for gd, nr in ((g_mix_d, 8), (g_ffn_d, 8), (g_ple_d, 8), (lnag_d, 16), (bgate_d, 16)):
            LD(vrows[r_:r_ + nr, :], gd.rearrange("(kc p) -> kc p", p=128), Bvrows)
            r_ += nr
        bkv, Bbkv = bank()
        fw.op("tensor", I("transpose", out=bkv[:, 0:56], in_=vrows, identity=identf[0:56, 0:56]), [Bvrows, Bidentf], [Bbkv], inc=True)
        V("tensor_copy", [Bbkv], [Bgcols], out=gcols[:].rearrange("p a b -> p (a b)"), in_=bkv[:, 0:24])
        V("tensor_copy", [Bbkv], [Blnag], out=lnag[:], in_=bkv[:, 24:40])
        V("tensor_copy", [Bbkv], [Bbgate], out=bgate[:], in_=bkv[:, 40:56])
        rel(Bbkv)
        LD(gfin[:], g_fin_d.unsqueeze(0).partition_broadcast(128), Bgfin)
        issue_upto(1 + PF)
        p0_first = P0_steps(0)
        p0_first[0]()
        wsf = R1[:, 0:2048].bitcast(F32).rearrange("p (g j) -> p g j", g=GA)
        Bwsf = Buf("wsf")
        wsb = R1[:, 2048:3072].rearrange("p (g j) -> p g j", g=GA)
        Bwsb = Buf("wsb")
        LD(wsf, ws_d.rearrange("g i j -> i g j"), Bwsf)
        V("tensor_tensor", [Bwsf, Btril], [Bwsb], out=wsb, in0=wsf, in1=tril[:].unsqueeze(1).broadcast_to([128, GA, 128]), op=ALU.mult)
        tb_, Btb_ = trbank()
        for g in range(GA):
            TR(tb_[:, g * 128:(g + 1) * 128], wsb[:, g, :], ident[:], [Bwsb, Bident], [Btb_], inc=(g == GA - 1))
        V("tensor_copy", [Btb_], [BwsT], out=wsT[:].rearrange("p g i -> p (g i)"), in_=tb_[:, 0:1024])
        rel(Btb_)
        ones_bf = R1[:, 3072:3073]; Bones = Buf("ones")
        G("memset", [], [Bones], ones_bf, 1.0)
        LBt = R1[0:2, 4096:8192].bitcast(F32)
        RBt = R1[0:2, 8192:10240].bitcast(F32)
        BLB = Buf("LB"); BRB = Buf("RB")
        V("memset", [], [BLB], LBt, 1.0)
        LD(LBt[0:1, :], lnab_d.unsqueeze(0), BLB)
        LD(RBt[1:2, :], bs_d.rearrange("g i -> (g i)").unsqueeze(0), BRB)
        for gh in range(2):
            bk, Bb = bank()
            for gg in range(4):
                g = gh * 4 + gg
                MM(bk[0:1, gg * 128:(gg + 1) * 128], ones_bf, wsT[:, g, :], True, True, [Bones, BwsT], [Bb], inc=(gg == 3))
            V("tensor_copy", [Bb], [BRB], out=RBt[0:1, gh * 512:(gh + 1) * 512], in_=bk[0:1, :])
            rel(Bb)
        for q4 in range(4):
            bk, Bb = bank()
            for cq in range(4):
                cc = q4 * 4 + cq
                g = cc // 2
                MM(bk[:, cq * 128:(cq + 1) * 128], LBt[:, cc * 128:(cc + 1) * 128], RBt[:, g * 128:(g + 1) * 128], True, True,
                   [BLB, BRB], [Bb], inc=(cq == 3))
            V("tensor_copy", [Bb], [BT2], out=T2[:, q4 * 4:(q4 + 1) * 4, :].rearrange("p a b -> p (a b)"), in_=bk[:])
            rel(Bb)
        for i in range(8):
            V("memset", [], [BS[i]], S[:, i, :], 0.0)
            G("memset", [], [BSbf[i]], Sbf[:, i, :], 0.0)
        fw.barrier()

        for blk in range(nblocks):
            t0 = blk * TB
            if blk == 0:
                fw.barrier(skip=("sync",))
            _so[0] = (blk % 2) * 512
            if blk == 0:
                for f_ in p0_first[1:]:
                    f_()
            for name, dstT, BdT in (("q", qT, BqT), ("k", kT, BkT)):
                for j in range(2):
                    pc, Bp = piece(name, j)
                    for hh in range(2):
                        h = 2 * j + hh
                        X1, B1 = bank(); X2, B2 = bank()
                        for half, (X, BX) in enumerate(((X1, B1), (X2, B2))):
                            c0 = hh * 256 + half * 128
                            for kc in range(8):
                                MM(X[:], pc[:, kc, c0:c0 + 128], xnT[:, kc, :], kc == 0, kc == 7, [Bp] + BxnT, [BX], inc=(kc == 7))
                        tm, Btm = (ropeA, [[b_] + Bsigt for b_ in BropeA]) if (h % 2 == 0) else (ropeB, [[b_] + Bxt for b_ in BropeB])
                        rope(X1[:], B1, X2[:], B2, dstT[:, 2 * h, :], dstT[:, 2 * h + 1, :], BdT[h], tm, Btm,
                             cst[:, 0, :], cst[:, 1, :], Bcst)
                        rel(B1); rel(B2)
            for t in range(NT):
                tb, Btb = trbank()
                for c in range(8):
                    TR(tb[:, c * 128:(c + 1) * 128], kT[:, c, t * 128:(t + 1) * 128], ident[:], BkT + [Bident], [Btb], inc=(c == 7))
                A("activation", [Btb], [Bk_tm[t]] + BhnT, out=k_tm[:, t, :], in_=tb[:], func=AF.Copy)
                rel(Btb)
            for j in range(4):
                pc, Bp = piece("v", j)
                for t in range(NT):
                    bk, Bb = bank()
                    for kc in range(8):
                        MM(bk[:], xnT[:, kc, t * 128:(t + 1) * 128], pc[:, kc, :], kc == 0, kc == 7, [Bp, BxnT[t]], [Bb], inc=(kc == 7))
                    A("activation", [Bb, Bkdec], [Bv_tm[t]] + BactT, out=v_tm[:, t, j * 512:(j + 1) * 512], in_=bk[:], func=AF.Copy,
                      scale=kdec[:, j:j + 1])
                    rel(Bb)
            for j in range(4):
                pc, Bp = piece("g", j)
                for t in range(NT):
                    bk, Bb = bank()
                    for kc in range(8):
                        MM(bk[:], xnT[:, kc, t * 128:(t + 1) * 128], pc[:, kc, :], kc == 0, kc == 7, [Bp, BxnT[t]], [Bb], inc=(kc == 7))
                    A("activation", [Bb], [Bsg[t]] + BactT + BpT + [Bptile, Bpbf], out=sg[:, t, j * 512:(j + 1) * 512], in_=bk[:], func=AF.Silu)
                    rel(Bb)
            def y_transposes(cq):
                csq = slice(cq * 128, (cq + 1) * 128)
                yb = ytm2[:, cq % 2, :]; Byb = Bytm2[cq % 2]
                for hf in range(2):
                    tb, Btb = trbank()
                    for c8 in range(8):
                        cc = hf * 8 + c8
                        TR(tb[:, c8 * 128:(c8 + 1) * 128], yb[:, cc * 128:(cc + 1) * 128], ident[:], [Byb, Bident], [Btb], inc=(c8 == 7))
                    A("activation", [Btb], [ByT[cq]] + Bhx, out=yT[:, hf * 8:(hf + 1) * 8, csq],
                      in_=tb[:].rearrange("p (a b) -> p a b", a=8), func=AF.Copy)
                    rel(Btb)

            for c in range(NT):
                cs_ = slice(c * 128, (c + 1) * 128)
                scb, Bsc = bank()
                for h in range(H):
                    for half in range(2):
                        MM(scb[:, h * 128:(h + 1) * 128], kT[:, 2 * h + half, cs_], qT[:, 2 * h + half, cs_], half == 0, half == 1,
                           [BkT[h], BqT[h]], [Bsc], inc=(h == H - 1 and half == 1))
                V("tensor_tensor", [Bsc, BmaskT], [BsTb], out=sTb[:], in0=scb[:], in1=maskT[:], op=ALU.mult)
                rel(Bsc)
                st6 = sm(24).rearrange("p (h s) -> p h s", h=H); mv4 = sm(8).rearrange("p (h s) -> p h s", h=H)
                sd4 = sm(4); rs4 = sm(4); nm4 = sm(4)
                Bst = Buf("st")
                for h in range(H):
                    hs = slice(h * 512, (h + 1) * 512)
                    ob, Bo = bank()
                    MM(ob[:], sTb[:, h * 128:(h + 1) * 128], v_tm[:, c, hs], True, False, [BsTb, Bv_tm[c]], [Bo], inc=False)
                    MM(ob[:], qT[:, 2 * h, cs_], Sbf[:, 2 * h, :], False, False, [BqT[h], BSbf[2 * h]], [Bo], inc=False)
                    MM(ob[:], qT[:, 2 * h + 1, cs_], Sbf[:, 2 * h + 1, :], False, True, [BqT[h], BSbf[2 * h + 1]], [Bo], inc=True)
                    A("activation", [Bo], [Bonf[h]], out=onf[:, h, :], in_=ob[:], func=AF.Copy)
                    rel(Bo)
                    for half in range(2):
                        ub, Bu = bank()
                        i8 = 2 * h + half
                        MM(ub[:], k_tm[:, c, h * 256 + half * 128: h * 256 + (half + 1) * 128], v_tm[:, c, hs], True, True,
                           [Bk_tm[c], Bv_tm[c]], [Bu], inc=True)
                        V("scalar_tensor_tensor", [BS[i8], Bu], [BS[i8]], out=S[:, i8, :], in0=S[:, i8, :], scalar=GAML[h],
                          in1=ub[:], op0=ALU.mult, op1=ALU.add)
                        rel(Bu)
                        if half == 0:
                            A("activation", [BS[i8]], [BSbf[i8]], out=Sbf[:, i8, :], in_=S[:, i8, :], func=AF.Copy)
                        else:
                            G("tensor_copy", [BS[i8]], [BSbf[i8]], out=Sbf[:, i8, :], in_=S[:, i8, :])
                    V("bn_stats", [Bonf[h]], [Bst], out=st6[:, h, :], in_=onf[:, h, :])
                pcg, Bpg_ = piece("gt", c)
                for mm in range(4):
                    m = 4 * c + mm
                    bk, Bb = bank()
                    for kc in range(8):
                        MM(bk[:], pcg[:, kc, mm * 128:(mm + 1) * 128], xnT[:, kc, :], kc == 0, kc == 7, [Bpg_] + BxnT, [Bb], inc=(kc == 7))
                    A("activation", [Bb, Bbgate], [Bgates[m]] + BropeA + BropeB + Bxt, out=gatesT_v[:, m, :], in_=bk[:], func=AF.Identity,
                      bias=bgate[:, m:m + 1])
                    rel(Bb)
                for h in range(H):
                    V("bn_aggr", [Bst], [Bst], out=mv4[:, h, :], in_=st6[:, h, :])
                V("tensor_tensor", [Bst, BepsT], [Bst], out=sd4, in0=mv4[:, :, 1], in1=epsT[:], op=ALU.add)
                A("activation", [Bst], [Bst], out=sd4, in_=sd4, func=AF.Sqrt)
                V("reciprocal", [Bst], [Bst], out=rs4, in_=sd4)
                V("scalar_tensor_tensor", [Bst], [Bst], out=nm4, in0=mv4[:, :, 0], scalar=-1.0, in1=rs4, op0=ALU.mult, op1=ALU.mult)
                yb = ytm2[:, c % 2, :]; Byb = Bytm2[c % 2]
                for h in range(H):
                    hs = slice(h * 512, (h + 1) * 512)
                    V("tensor_scalar", [Bonf[h], Bst], [Bonf[h]], out=onf[:, h, :], in0=onf[:, h, :], scalar1=rs4[:, h:h + 1],
                      scalar2=nm4[:, h:h + 1], op0=ALU.mult, op1=ALU.add)
                    G("tensor_tensor", [Bonf[h], Bsg[c]], [Byb], out=yb[:, hs], in0=onf[:, h, :], in1=sg[:, c, hs], op=ALU.mult)
                if c > 0:
                    y_transposes(c - 1)
            y_transposes(NT - 1)
            if blk == nblocks - 1:
                for i8 in range(8):
                    h, half = i8 // 2, i8 % 2
                    oi = i8 % 2
                    A("activation", [BS[i8]], [Bonf[oi]], out=onf[:, oi, :], in_=S[:, i8, :], func=AF.Copy, scale=1.0 / 16.0)
                    ST(sp_d[h, half * 128:(half + 1) * 128, :], onf[:, oi, :], Bonf[oi])
            fw.barrier(skip=("sync",))
            pstage = cst[:].rearrange("p a b -> p (a b)")
            for t in range(NT):
                LD(pstage[:, t * 256:(t + 1) * 256], p_d[t0 + t * 128: t0 + (t + 1) * 128, :], Bcst)
            stv = sm(NT * 4 * 6); Bstv = [Buf(f"stv{t}") for t in range(NT)]
            stv = stv.rearrange("p (t j s) -> p t j s", t=NT, j=4)
            for j in range(4):
                pc, Bp = piece("va", j)
                for t in range(NT):
                    bk, Bb = bank()
                    for kc in range(8):
                        MM(bk[:], xnT[:, kc, t * 128:(t + 1) * 128], pc[:, kc, :], kc == 0, kc == 7, [Bp, BxnT[t]], [Bb], inc=(kc == 7))
                    A("activation", [Bb], [Bn_tm[t]], out=n_tm[:, t, j * 512:(j + 1) * 512], in_=bk[:], func=AF.Gelu)
                    rel(Bb)
                    V("bn_stats", [Bn_tm[t]], [Bstv[t]], out=stv[:, t, j, :], in_=n_tm[:, t, j * 512:(j + 1) * 512])
            for j in range(4):
                pc, Bp = piece("u", j)
                for mm in range(4):
                    m = 4 * j + mm
                    bk, Bb = bank()
                    for kc in range(8):
                        MM(bk[:], pc[:, kc, mm * 128:(mm + 1) * 128], xnT[:, kc, :], kc == 0, kc == 7, [Bp] + BxnT, [Bb], inc=(kc == 7))
                    A("activation", [Bb], [BuT[m]], out=uT[:, m, :], in_=bk[:], func=AF.Gelu)
                    rel(Bb)
            for t in range(NT):
                mv = sm(2); sd = sm(1); rs = sm(1); nmr = sm(1)
                V("bn_aggr", [Bstv[t]], [Bstv[t]], out=mv, in_=stv[:, t, :, :].rearrange("p j s -> p (j s)"))
                A("activation", [Bstv[t]], [Bstv[t]], out=sd, in_=mv[:, 1:2], func=AF.Sqrt, bias=EPS)
                V("reciprocal", [Bstv[t]], [Bstv[t]], out=rs, in_=sd)
                V("scalar_tensor_tensor", [Bstv[t]], [Bstv[t]], out=nmr, in0=mv[:, 0:1], scalar=-1.0, in1=rs, op0=ALU.mult, op1=ALU.mult)
                V("tensor_scalar", [Bn_tm[t], Bstv[t]], [Bn_tm[t]], out=n_tm[:, t, :], in0=n_tm[:, t, :], scalar1=rs, scalar2=nmr,
                  op0=ALU.mult, op1=ALU.add)
            for cc in range(16):
                g = cc // 2
                bk, Bb = bank()
                for tc_ in range(NT):
                    MM(bk[:, tc_ * 128:(tc_ + 1) * 128], n_tm[:, tc_, cc * 128:(cc + 1) * 128], wsT[:, g, :], True, True,
                       [Bn_tm[tc_], BwsT], [Bb], inc=(tc_ == NT - 1))
                ti = cc % 2
                V("scalar_tensor_tensor", [Bb, Blnag, BT2], [BtmpA[ti]], out=tmpA[:, ti, :].rearrange("p (a b) -> p a b", a=NT),
                  in0=bk[:].rearrange("p (a b) -> p a b", a=NT), scalar=lnag[:, cc:cc + 1],
                  in1=T2[:, cc, :].unsqueeze(1).broadcast_to([128, NT, 128]), op0=ALU.mult, op1=ALU.add)
                rel(Bb)
                G("tensor_tensor", [BtmpA[ti], BuT[cc]], [BuT[cc]], out=uT[:, cc, :], in0=tmpA[:, ti, :], in1=uT[:, cc, :], op=ALU.mult)
            for m in range(16):
                mo = (m // 2) + 8 * (m % 2)
                A("activation", [Bgates[mo]], [Bgates[mo]], out=gatesT_v[:, mo, :], in_=gatesT_v[:, mo, :], func=AF.Sigmoid)
            for j in range(4):
                pa, Bpa = piece("ao", j)
                pr, Bpr = piece("ro", j)
                for mm in range(2):
                    m = 2 * j + mm
                    ba, Bba = bank(); bb, Bbb = bank()
                    for kc in range(16):
                        MM(ba[:], pa[:, kc, mm * 128:(mm + 1) * 128], uT[:, kc, :], kc == 0, kc == 15, [Bpa] + BuT, [Bba], inc=(kc == 15))
                    for kc in range(16):
                        MM(bb[:], pr[:, kc, mm * 128:(mm + 1) * 128], yT[:, kc, :], kc == 0, kc == 15, [Bpr] + ByT, [Bbb], inc=(kc == 15))
                    ti = m % 2
                    V("tensor_tensor", [Bba, Bgates[m]], [BtmpA[ti]], out=tmpA[:, ti, :], in0=ba[:], in1=gatesT_v[:, m, :], op=ALU.mult)
                    V("tensor_tensor", [Bbb, Bgates[8 + m]], [BtmpB[ti]], out=tmpB[:, ti, :], in0=bb[:], in1=gatesT_v[:, 8 + m, :], op=ALU.mult)
                    rel(Bba); rel(Bbb)
                    G("tensor_tensor", [BtmpA[ti], BtmpB[ti]], [Bmerged[m]], out=mergedT[:, m, :], in0=tmpA[:, ti, :], in1=tmpB[:, ti, :], op=ALU.add)
            fw.barrier(skip=("sync",))
            for t in range(NT):
                LD(hx[:, t, :], x_d[t0 + t * 128: t0 + (t + 1) * 128, :], Bhx[t])
            for t in range(NT):
                pst = pstage[:, t * 256:(t + 1) * 256]; pbb = xnb[:, t % 2, 0:256]
                V("tensor_copy", [Bcst], [Bxnb[t % 2]], out=pbb, in_=pst)
                tb, Btb = trbank()
                for kc in range(2):
                    TR(tb[:, kc * 128:(kc + 1) * 128], pbb[:, kc * 128:(kc + 1) * 128], ident[:], [Bxnb[t % 2], Bident], [Btb], inc=(kc == 1))
                V("tensor_copy", [Btb], [BpT[t]], out=pT[:, :, t * 128:(t + 1) * 128], in_=tb[:, 0:256].rearrange("p (a b) -> p a b", a=2))
                rel(Btb)
            for j in range(2):
                pc, Bp = piece("wo", j)
                for t in range(NT):
                    bk, Bb = bank()
                    for kc in range(8):
                        MM(bk[:], mergedT[:, kc, t * 128:(t + 1) * 128], pc[:, kc, :], kc == 0, kc == 7, [Bp] + Bmerged, [Bb], inc=(kc == 7))
                    V("tensor_tensor", [Bb, Bhx[t]], [Bhx[t]], out=hx[:, t, j * 512:(j + 1) * 512], in0=bk[:], in1=hx[:, t, j * 512:(j + 1) * 512], op=ALU.add)
                    rel(Bb)
            rms_pipe([hx[:, t, :] for t in range(NT)], Bhx, 1, [hnT[:, :, t * 128:(t + 1) * 128] for t in range(NT)], BhnT)
            p0s = P0_steps(blk + 1) if blk + 1 < nblocks else []
            for j in range(6):
                if j < len(p0s):
                    p0s[j]()
                pg, Bpg = piece("fg", j)
                pu, Bpu = piece("fu", j)
                for mm in range(pg.shape[2] // 128):
                    m = 4 * j + mm
                    bg, Bbg = bank(); bu, Bbu = bank()
                    for kc in range(8):
                        MM(bg[:], pg[:, kc, mm * 128:(mm + 1) * 128], hnT[:, kc, :], kc == 0, kc == 7, [Bpg] + BhnT, [Bbg], inc=(kc == 7))
                    for kc in range(8):
                        MM(bu[:], pu[:, kc, mm * 128:(mm + 1) * 128], hnT[:, kc, :], kc == 0, kc == 7, [Bpu] + BhnT, [Bbu], inc=(kc == 7))
                    si = m % 4
                    A("activation", [Bbg], [Bsigt[si]], out=sigt[si], in_=bg[:], func=AF.Silu)
                    rel(Bbg)
                    V("tensor_tensor", [Bbu, Bsigt[si]], [BactT[m]], out=actT[:, m, :], in0=bu[:], in1=sigt[si], op=ALU.mult)
                    rel(Bbu)
            for c2 in range(2):
                db = [bank() for _ in range(NT)]
                for kp in range(3):
                    pc, Bp = piece("fd", c2 * 3 + kp)
                    nk = pc.shape[1]
                    for t in range(NT):
                        for kc in range(nk):
                            m = kp * 8 + kc
                            MM(db[t][0][:], actT[:, m, t * 128:(t + 1) * 128], pc[:, kc, :], (kp == 0 and kc == 0), (kp == 2 and kc == nk - 1),
                               [Bp, BactT[m]], [db[t][1]], inc=(kc == nk - 1))
                for t in range(NT):
                    V("tensor_tensor", [db[t][1], Bhx[t]], [Bhx[t]], out=hx[:, t, c2 * 512:(c2 + 1) * 512], in0=db[t][0][:],
                      in1=hx[:, t, c2 * 512:(c2 + 1) * 512], op=ALU.add)
                    rel(db[t][1])
            rms_pipe([hx[:, t, :] for t in range(NT)], Bhx, 2, [hnT[:, :, t * 128:(t + 1) * 128] for t in range(NT)], BhnT)
            for j in range(2):
                pg, Bpg = piece("pg", j)
                pl, Bpl = piece("pl", j)
                for t in range(NT):
                    bg, Bbg = bank(); bl, Bbl = bank()
                    for kc in range(8):
                        MM(bg[:], hnT[:, kc, t * 128:(t + 1) * 128], pg[:, kc, :], kc == 0, kc == 7, [Bpg, BhnT[t]], [Bbg], inc=(kc == 7))
                    for kc in range(2):
                        MM(bl[:], pT[:, kc, t * 128:(t + 1) * 128], pl[:, kc, :], kc == 0, kc == 1, [Bpl, BpT[t]], [Bbl], inc=(kc == 1))
                    si = t % 4
                    A("activation", [Bbg], [Bsigt[si]], out=sigt[si], in_=bg[:], func=AF.Sigmoid)
                    rel(Bbg)
                    V("tensor_tensor", [Bbl, Bsigt[si]], [Bsigt[si]], out=sigt[si], in0=bl[:], in1=sigt[si], op=ALU.mult)
                    rel(Bbl)
                    G("tensor_tensor", [Bsigt[si], Bhx[t]], [Bhx[t]], out=hx[:, t, j * 512:(j + 1) * 512], in0=sigt[si],
                      in1=hx[:, t, j * 512:(j + 1) * 512], op=ALU.add)
            for t in range(NT):
                ssq = sm(1); rt = sm(1); rs = sm(1); Bq = Buf("fq")
                xb = xnb[:, t % 2, :]; Bxb = Bxnb[t % 2]
                A("activation", [Bhx[t]], [Bxb, Bq], out=xb, in_=hx[:, t, :], func=AF.Square, accum_out=ssq)
                A("activation", [Bq], [Bq], out=rt, in_=ssq, func=AF.Sqrt, scale=1.0 / D, bias=EPS)
                V("reciprocal", [Bq], [Bq], out=rs, in_=rt)
                V("scalar_tensor_tensor", [Bhx[t], Bq, Bgfin], [Bhx[t]], out=hx[:, t, :], in0=hx[:, t, :], scalar=rs, in1=gfin[:],
                  op0=ALU.mult, op1=ALU.mult)
                ST(y_d[t0 + t * 128: t0 + (t + 1) * 128, :], hx[:, t, :], Bhx[t])

        if do_sample:
            fw.barrier()
            _so[0] = 0
            P = NS
            def sv(off, n, dt=F32):
                if dt == F32:
                    return R1[0:P, off // 2: off // 2 + 2 * n].bitcast(F32)
                return R1[0:P, off // 2: off // 2 + n]
            o = 0
            xs_t = sv(o, 1024); o += 4096
            q_s = sv(o, 1024); o += 4096
            k_s = sv(o, 1024); o += 4096
            v_s = sv(o, 2048); o += 8192
            sg_s = sv(o, 2048); o += 8192
            u_s = sv(o, 2048); o += 8192
            gv_s = sv(o, 2048); o += 8192
            gt_s = sv(o, 2048); o += 8192
            vbf_s = sv(o, 2048, BF16); o += 4096
            big_bf = xnb[0:P].rearrange("p a b -> p (a b)")
            Bxs = Buf("xs"); Bq_s = Buf("q_s"); Bk_s = Buf("k_s"); Bv_s = Buf("v_s"); Bsg_s = Buf("sg_s")
            Bu_s = Buf("u_s"); Bgv_s = Buf("gv_s"); Bgt_s = Buf("gt_s"); Bvbf = Buf("vbf"); Bbig = Buf("bigbf")
            yTf = yT[:].rearrange("p a b -> p (a b)")
            xnT_s = yTf[:, 0:128].rearrange("p (a b) -> p a b", a=8); BxnT_s = Buf("xnT_s")
            qT_s = yTf[:, 128:256].rearrange("p (a b) -> p a b", a=8); BqT_s = Buf("qT_s")
            yT_s = yTf[:, 256:512].rearrange("p (a b) -> p a b", a=16); ByT_s = Buf("yT_s")
            usT_s = yTf[:, 512:768].rearrange("p (a b) -> p a b", a=16); BusT_s = Buf("usT_s")
            mgT_s = yTf[:, 768:896].rearrange("p (a b) -> p a b", a=8); BmgT_s = Buf("mgT_s")
            hnT_s = yTf[:, 896:1024].rearrange("p (a b) -> p a b", a=8); BhnT_s = Buf("hnT_s")
            actT_s = yTf[:, 1024:1376].rearrange("p (a b) -> p a b", a=22); BactT_s = Buf("actT_s")
            pT_s = yTf[:, 1376:1408].rearrange("p (a b) -> p a b", a=2); BpT_s = Buf("pT_s")
            qbd = yTf[:, 2048:4096].rearrange("p (c b k) -> p c b k", c=8, b=16); Bqbd = Buf("qbd")
            s0bf = [yTf[:, 4096 + i * 1024: 4096 + (i + 1) * 1024].rearrange("p (a b) -> p a b", a=2) for i in range(2)]
            Bs0bf = [Buf("s0bf0"), Buf("s0bf1")]
            kbd = xnT[0:P].rearrange("p a b -> p (a b)")
            Bkbd = Buf("kbd")
            r0s = R0[0:P, :]
            tA = r0s[:, 0:1024]; tB = r0s[:, 1024:2048]; tC = r0s[:, 2048:3072]; tD = r0s[:, 3072:4096]
            BtA = Buf("tA"); BtB = Buf("tB"); BtC = Buf("tC"); BtD = Buf("tD")
            stt = [S[:, 2 * i:2 * i + 2, :] for i in range(4)]
            stt += [R1[:, (28672 + i * 4096) // 2:(28672 + (i + 1) * 4096) // 2].bitcast(F32).rearrange("p (a b) -> p a b", a=2) for i in range(4)]
            NST = len(stt)
            Bstt = [Buf(f"stt{i}") for i in range(NST)]
            cs_s = cst[0:P, 0, 0:128]; sn_s = cst[0:P, 0, 128:256]; Bcs_s = Buf("cs_s")
            ws00 = cst[0:P, 0, 256:264]; bs0 = cst[0:P, 0, 264:272]; Bws00 = Buf("ws00")
            bcA = Sbf[0:P].rearrange("p a b -> p (a b)").bitcast(F32)
            BbcA = Buf("bcA")
            h_s = onf[0:P, 0:2, :].rearrange("p a b -> p (a b)"); Bh_s = Buf("h_s")
            y_s = ytm2[0:P, 0, :]; By_s = Buf("y_s")

            LD(xs_t, xs_d, Bxs)
            LD(cs_s, cs_s_d.partition_broadcast(P), Bcs_s); LD(sn_s, sn_s_d.partition_broadcast(P), Bcs_s)
            LD(ws00, ws_d[:, 0:1, 0:1].rearrange("g a b -> (a b) g").partition_broadcast(P), Bws00, allow_slow_non_contiguous=True)
            LD(bs0, bs_d[:, 0:1].rearrange("g a -> a g").partition_broadcast(P), Bws00, allow_slow_non_contiguous=True)

            def rms_s(src, Bsrc, gi, dstT, Bdst):
                ssq = sm(1); rt = sm(1); rs = sm(1); Bq = Buf("sq")
                xb = big_bf[:, 0:1024]
                A("activation", [Bsrc], [Bbig, Bq], out=xb, in_=src, func=AF.Square, accum_out=ssq[0:P, :])
                A("activation", [Bq], [Bq], out=rt[0:P, :], in_=ssq[0:P, :], func=AF.Sqrt, scale=1.0 / D, bias=EPS)
                V("reciprocal", [Bq], [Bq], out=rs[0:P, :], in_=rt[0:P, :])
                V("tensor_scalar", [Bsrc, Bq], [Bbig], out=xb, in0=src, scalar1=rs[0:P, :], scalar2=None, op0=ALU.mult)
                tb, Btb = trbank()
                tv = tb[:, 0:8 * P].rearrange("p (a b) -> p a b", a=8)
                for kc in range(8):
                    TR(tv[:, kc, :], xb[:, kc * 128:(kc + 1) * 128], ident[0:P, 0:P], [Bbig, Bident], [Btb], inc=(kc == 7))
                V("tensor_tensor", [Btb, Bgcols], [Bdst], out=dstT, in0=tv, in1=gcols[:, gi, :].unsqueeze(2).broadcast_to([128, 8, P]), op=ALU.mult)
                rel(Btb)
                return rs

            def toT(src_bf, Bsrc, n, dstT, Bdst):
                tb, Btb = trbank()
                tv = tb[:, 0:n * P].rearrange("p (a b) -> p a b", a=n)
                for kc in range(n):
                    TR(tv[:, kc, :], src_bf[:, kc * 128:(kc + 1) * 128], ident[0:P, 0:P], [Bsrc, Bident], [Btb], inc=(kc == n - 1))
                V("tensor_copy", [Btb], [Bdst], out=dstT, in_=tv)
                rel(Btb)

            def proj(pc, Bp, actT_, BaT, nk, ncol):
                bk, Bb = bank()
                for kc in range(nk):
                    MM(bk[0:P, 0:ncol], actT_[:, kc, :], pc[:, kc, :], kc == 0, kc == nk - 1, [Bp, BaT], [Bb], inc=(kc == nk - 1))
                return bk, Bb

            rms_s(xs_t, Bxs, 0, xnT_s, BxnT_s)
            for name, dst, Bd, fn in (("q", q_s, Bq_s, AF.Copy), ("k", k_s, Bk_s, AF.Copy)):
                for j in range(2):
                    pc, Bp = piece(name, j)
                    bk, Bb = proj(pc, Bp, xnT_s, BxnT_s, 8, 512)
                    A("activation", [Bb], [Bd], out=dst[:, j * 512:(j + 1) * 512], in_=bk[0:P, :], func=fn)
                    rel(Bb)
            for name, dst, Bd, fn in (("v", v_s, Bv_s, AF.Copy), ("g", sg_s, Bsg_s, AF.Silu)):
                for j in range(4):
                    pc, Bp = piece(name, j)
                    bk, Bb = proj(pc, Bp, xnT_s, BxnT_s, 8, 512)
                    A("activation", [Bb], [Bd], out=dst[:, j * 512:(j + 1) * 512], in_=bk[0:P, :], func=fn)
                    rel(Bb)
            LD(bcA, bgate_d.unsqueeze(0).partition_broadcast(P), BbcA)
            for j in range(4):
                pc, Bp = piece("gt", j)
                bk, Bb = proj(pc, Bp, xnT_s, BxnT_s, 8, 512)
                V("tensor_tensor", [Bb, BbcA], [Bgt_s], out=gt_s[:, j * 512:(j + 1) * 512], in0=bk[0:P, :], in1=bcA[:, j * 512:(j + 1) * 512], op=ALU.add)
                rel(Bb)
            A("activation", [Bgt_s], [Bgt_s], out=gt_s, in_=gt_s, func=AF.Sigmoid)
            def rope_s(src, Bsrc, scale, dst, Bdst):
                s4 = src.rearrange("p (h t d) -> p h t d", h=H, t=2)
                d4 = dst.rearrange("p (h t d) -> p h t d", h=H, t=2)
                x1 = s4[:, :, 0, :]; x2 = s4[:, :, 1, :]
                cb = cs_s.unsqueeze(1).broadcast_to([P, H, 128]); sb = sn_s.unsqueeze(1).broadcast_to([P, H, 128])
                ta = tA[:, 0:512].rearrange("p (h d) -> p h d", h=H); tb_ = tA[:, 512:1024].rearrange("p (h d) -> p h d", h=H)
                tc = tB[:, 0:512].rearrange("p (h d) -> p h d", h=H); td = tB[:, 512:1024].rearrange("p (h d) -> p h d", h=H)
                V("scalar_tensor_tensor", [Bsrc, Bcs_s], [BtA], out=ta, in0=x1, scalar=scale, in1=cb, op0=ALU.mult, op1=ALU.mult)
                V("scalar_tensor_tensor", [Bsrc, Bcs_s], [BtA], out=tb_, in0=x2, scalar=scale, in1=sb, op0=ALU.mult, op1=ALU.mult)
                V("scalar_tensor_tensor", [Bsrc, Bcs_s], [BtB], out=tc, in0=x1, scalar=scale, in1=sb, op0=ALU.mult, op1=ALU.mult)
                V("scalar_tensor_tensor", [Bsrc, Bcs_s], [BtB], out=td, in0=x2, scalar=scale, in1=cb, op0=ALU.mult, op1=ALU.mult)
                V("tensor_tensor", [BtA, Bsrc], [Bdst], out=d4[:, :, 0, :], in0=ta, in1=tb_, op=ALU.subtract)
                V("tensor_tensor", [BtB, Bsrc], [Bdst], out=d4[:, :, 1, :], in0=tc, in1=td, op=ALU.add)
            rope_s(q_s, Bq_s, 1.0, tC, BtC)
            V("tensor_copy", [BtC], [Bq_s], out=q_s, in_=tC)
            rope_s(k_s, Bk_s, 1.0 / 16.0, tC, BtC)
            V("tensor_copy", [BtC], [Bk_s], out=k_s, in_=tC)
            qk = sm(H); Bqk = Buf("qk")
            V("tensor_tensor", [Bq_s, Bk_s], [BtC], out=tC, in0=q_s, in1=k_s, op=ALU.mult)
            V("tensor_reduce", [BtC], [Bqk], out=qk[0:P, :], in_=tC.rearrange("p (h d) -> p h d", h=H), axis=AX.X, op=ALU.add)
            V("tensor_copy", [Bv_s], [Bvbf], out=vbf_s, in_=v_s)
            V("tensor_copy", [Bq_s], [Bbig], out=big_bf[:, 0:1024], in_=q_s)
            toT(big_bf[:, 0:1024], Bbig, 8, qT_s, BqT_s)
            V("tensor_tensor", [BqT_s, Bi16], [Bqbd], out=qbd, in0=i16[:].unsqueeze(1).broadcast_to([128, 8, 16, 16]),
              in1=qT_s.unsqueeze(3).broadcast_to([128, 8, 16, 16]), op=ALU.mult)
            it = 0
            for h in range(H):
                V("tensor_tensor", [Bk_s, Bident], [Bkbd], out=kbd.rearrange("p (b d) -> p b d", b=16),
                  in0=k_s[:, h * 256:(h + 1) * 256].unsqueeze(1).broadcast_to([P, 16, 256]),
                  in1=ident[0:P, 0:16].unsqueeze(2).broadcast_to([P, 16, 256]), op=ALU.mult)
                ocb, Boc = bank()
                for b in range(NS):
                    si = it % NST; it += 1
                    LD(stt[si], s0_d[b, h].rearrange("(half p) v -> p half v", p=128), Bstt[si], q="sync")
                    ci = it % 2
                    A("activation", [Bstt[si]], [Bs0bf[ci]], out=s0bf[ci], in_=stt[si], func=AF.Copy)
                    for half in range(2):
                        MM(ocb[0:P, :], qbd[:, 2 * h + half, b, :], s0bf[ci][:, half, :], (b == 0 and half == 0), (b == NS - 1 and half == 1),
                           [Bqbd, Bs0bf[ci]], [Boc], inc=(half == 1))
                    for half in range(2):
                        ub, Bu = bank()
                        MM(ub[:], kbd[:, b * 256 + half * 128: b * 256 + (half + 1) * 128], vbf_s[:, h * 512:(h + 1) * 512], True, True,
                           [Bkbd, Bvbf], [Bu], inc=True)
                        V("scalar_tensor_tensor", [Bstt[si], Bu], [Bstt[si]], out=stt[si][:, half, :], in0=stt[si][:, half, :], scalar=GAM[h],
                          in1=ub[:], op0=ALU.mult, op1=ALU.add)
                        rel(Bu)
                    fw.dma("gpsimd", I("dma_start", out=sso_d[b, h].rearrange("(half p) v -> p half v", p=128), in_=stt[si]),
                           reads=[Bstt[si]], is_output=True)
                oh = tD[:, 0:512]
                V("tensor_scalar", [Boc], [BtD], out=oh, in0=ocb[0:P, :], scalar1=GAM[h], scalar2=None, op0=ALU.mult)
                rel(Boc)
                V("scalar_tensor_tensor", [Bv_s, Bqk, BtD], [BtD], out=oh, in0=v_s[:, h * 512:(h + 1) * 512], scalar=qk[0:P, h:h + 1], in1=oh,
                  op0=ALU.mult, op1=ALU.add)
                st6 = sm(6); mv = sm(2); sd = sm(1); rs = sm(1); nmr = sm(1); Bst = Buf("sst")
                V("bn_stats", [BtD], [Bst], out=st6[0:P, :], in_=oh)
                V("bn_aggr", [Bst], [Bst], out=mv[0:P, :], in_=st6[0:P, :])
                A("activation", [Bst], [Bst], out=sd[0:P, :], in_=mv[0:P, 1:2], func=AF.Sqrt, bias=EPS)
                V("reciprocal", [Bst], [Bst], out=rs[0:P, :], in_=sd[0:P, :])
                V("scalar_tensor_tensor", [Bst], [Bst], out=nmr[0:P, :], in0=mv[0:P, 0:1], scalar=-1.0, in1=rs[0:P, :], op0=ALU.mult, op1=ALU.mult)
                V("tensor_scalar", [BtD, Bst], [BtD], out=oh, in0=oh, scalar1=rs[0:P, :], scalar2=nmr[0:P, :], op0=ALU.mult, op1=ALU.add)
                V("tensor_tensor", [BtD, Bsg_s], [By_s], out=y_s[:, h * 512:(h + 1) * 512], in0=oh, in1=sg_s[:, h * 512:(h + 1) * 512], op=ALU.mult)
            toT(y_s, By_s, 16, yT_s, ByT_s)
            fw.barrier()
            for name, dst, Bd, fn in (("va", gv_s, Bgv_s, AF.Gelu), ("u", u_s, Bu_s, AF.Gelu)):
                for j in range(4):
                    pc, Bp = piece(name, j)
                    bk, Bb = proj(pc, Bp, xnT_s, BxnT_s, 8, 512)
                    A("activation", [Bb], [Bd], out=dst[:, j * 512:(j + 1) * 512], in_=bk[0:P, :], func=fn)
                    rel(Bb)
            stq = sm(24); mv = sm(2); sd = sm(1); rs = sm(1); nmr = sm(1); Bsq = Buf("stq")
            for j in range(4):
                V("bn_stats", [Bgv_s], [Bsq], out=stq[0:P, j * 6:(j + 1) * 6], in_=gv_s[:, j * 512:(j + 1) * 512])
            V("bn_aggr", [Bsq], [Bsq], out=mv[0:P, :], in_=stq[0:P, :])
            A("activation", [Bsq], [Bsq], out=sd[0:P, :], in_=mv[0:P, 1:2], func=AF.Sqrt, bias=EPS)
            V("reciprocal", [Bsq], [Bsq], out=rs[0:P, :], in_=sd[0:P, :])
            V("scalar_tensor_tensor", [Bsq], [Bsq], out=nmr[0:P, :], in0=mv[0:P, 0:1], scalar=-1.0, in1=rs[0:P, :], op0=ALU.mult, op1=ALU.mult)
            V("tensor_scalar", [Bgv_s, Bsq], [Bgv_s], out=gv_s, in0=gv_s, scalar1=rs[0:P, :], scalar2=nmr[0:P, :], op0=ALU.mult, op1=ALU.add)
            LD(bcA, lnag_d.unsqueeze(0).partition_broadcast(P), BbcA)
            V("tensor_tensor", [Bgv_s, BbcA], [Bgv_s], out=gv_s, in0=gv_s, in1=bcA, op=ALU.mult)
            LD(bcA, lnab_d.unsqueeze(0).partition_broadcast(P), BbcA)
            V("tensor_tensor", [Bgv_s, BbcA], [Bgv_s], out=gv_s, in0=gv_s, in1=bcA, op=ALU.add)
            ST(cv_d, gv_s, Bgv_s)
            for g in range(GA):
                gs = slice(g * 256, (g + 1) * 256)
                V("tensor_scalar", [Bgv_s, Bws00], [BtC], out=tC[:, 0:256], in0=gv_s[:, gs], scalar1=ws00[:, g:g + 1], scalar2=bs0[:, g:g + 1],
                  op0=ALU.mult, op1=ALU.add)
                V("tensor_tensor", [BtC, Bu_s], [Bbig], out=big_bf[:, gs], in0=tC[:, 0:256], in1=u_s[:, gs], op=ALU.mult)
            toT(big_bf, Bbig, 16, usT_s, BusT_s)
            for j in range(4):
                pa, Bpa = piece("ao", j)
                pr, Bpr = piece("ro", j)
                ba, Bba = proj(pa, Bpa, usT_s, BusT_s, 16, 256)
                bb, Bbb = proj(pr, Bpr, yT_s, ByT_s, 16, 256)
                cs2 = slice(j * 256, (j + 1) * 256)
                V("tensor_tensor", [Bba, Bgt_s], [BtA], out=tA[:, cs2], in0=ba[0:P, 0:256], in1=gt_s[:, cs2], op=ALU.mult)
                V("tensor_tensor", [Bbb, Bgt_s], [BtB], out=tB[:, cs2], in0=bb[0:P, 0:256], in1=gt_s[:, 1024 + j * 256: 1024 + (j + 1) * 256], op=ALU.mult)
                rel(Bba); rel(Bbb)
                V("tensor_tensor", [BtA, BtB], [Bbig], out=big_bf[:, cs2], in0=tA[:, cs2], in1=tB[:, cs2], op=ALU.add)
            toT(big_bf[:, 0:1024], Bbig, 8, mgT_s, BmgT_s)
            for j in range(2):
                pc, Bp = piece("wo", j)
                bk, Bb = proj(pc, Bp, mgT_s, BmgT_s, 8, 512)
                V("tensor_tensor", [Bb, Bxs], [Bh_s], out=h_s[:, j * 512:(j + 1) * 512], in0=bk[0:P, :], in1=xs_t[:, j * 512:(j + 1) * 512], op=ALU.add)
                rel(Bb)
            rms_s(h_s, Bh_s, 1, hnT_s, BhnT_s)
            act_s = u_s.bitcast(BF16)[:, 0:DFF]
            for j in range(6):
                pg, Bpg = piece("fg", j)
                pu, Bpu = piece("fu", j)
                ncol = pg.shape[2]
                bg, Bbg = proj(pg, Bpg, hnT_s, BhnT_s, 8, ncol)
                bu, Bbu = proj(pu, Bpu, hnT_s, BhnT_s, 8, ncol)
                A("activation", [Bbg], [BtC], out=tC[:, 0:ncol], in_=bg[0:P, 0:ncol], func=AF.Silu)
                rel(Bbg)
                V("tensor_tensor", [Bbu, BtC], [Bu_s], out=act_s[:, j * 512: j * 512 + ncol], in0=bu[0:P, 0:ncol], in1=tC[:, 0:ncol], op=ALU.mult)
                rel(Bbu)
            toT(act_s[:, 0:2048], Bu_s, 16, actT_s[:, 0:16, :], BactT_s)
            toT(act_s[:, 2048:DFF], Bu_s, 6, actT_s[:, 16:22, :], BactT_s)
            for c2 in range(2):
                bk, Bb = bank()
                for kp in range(3):
                    pc, Bp = piece("fd", c2 * 3 + kp)
                    nk = pc.shape[1]
                    for kc in range(nk):
                        MM(bk[0:P, :], actT_s[:, kp * 8 + kc, :], pc[:, kc, :], (kp == 0 and kc == 0), (kp == 2 and kc == nk - 1), [Bp, BactT_s], [Bb],
                           inc=(kc == nk - 1))
                V("tensor_tensor", [Bb, Bh_s], [Bh_s], out=h_s[:, c2 * 512:(c2 + 1) * 512], in0=bk[0:P, :], in1=h_s[:, c2 * 512:(c2 + 1) * 512], op=ALU.add)
                rel(Bb)
            ps_t = tD[:, 0:256]
            LD(ps_t, ps_d, BtD)
            V("tensor_copy", [BtD], [Bbig], out=big_bf[:, 0:256], in_=ps_t)
            toT(big_bf[:, 0:256], Bbig, 2, pT_s, BpT_s)
            rms_s(h_s, Bh_s, 2, hnT_s, BhnT_s)
            for j in range(2):
                pg, Bpg = piece("pg", j)
                pl, Bpl = piece("pl", j)
                bg, Bbg = proj(pg, Bpg, hnT_s, BhnT_s, 8, 512)
                bl, Bbl = proj(pl, Bpl, pT_s, BpT_s, 2, 512)
                A("activation", [Bbg], [BtC], out=tC[:, 0:512], in_=bg[0:P, :], func=AF.Sigmoid)
                rel(Bbg)
                V("tensor_tensor", [Bbl, BtC], [BtC], out=tC[:, 0:512], in0=bl[0:P, :], in1=tC[:, 0:512], op=ALU.mult)
                rel(Bbl)
                V("tensor_tensor", [BtC, Bh_s], [Bh_s], out=h_s[:, j * 512:(j + 1) * 512], in0=tC[:, 0:512], in1=h_s[:, j * 512:(j + 1) * 512], op=ALU.add)
            ssq = sm(1); rt = sm(1); rs = sm(1); Bq = Buf("fqs")
            A("activation", [Bh_s], [Bbig, Bq], out=big_bf[:, 0:1024], in_=h_s, func=AF.Square, accum_out=ssq[0:P, :])
            A("activation", [Bq], [Bq], out=rt[0:P, :], in_=ssq[0:P, :], func=AF.Sqrt, scale=1.0 / D, bias=EPS)
            V("reciprocal", [Bq], [Bq], out=rs[0:P, :], in_=rt[0:P, :])
            V("scalar_tensor_tensor", [Bh_s, Bq, Bgfin], [Bh_s], out=h_s, in0=h_s, scalar=rs[0:P, :], in1=gfin[0:P, :], op0=ALU.mult, op1=ALU.mult)
            ST(ys_d, h_s, Bh_s)

        assert wstate["next"] == len(schedule), (wstate, len(schedule))
        fw.finish()
        fw.replay()
        build_program.stats = dict(fw.n_inst)
    return nc


def _constants():
    half = 128
    freqs = (np.float32(10000.0) ** (-np.arange(half, dtype=np.float32) / np.float32(half))).astype(np.float32)
    pos = np.arange(SEQ, dtype=np.float32)
    ang = (pos[None, :] * freqs[:, None]).astype(np.float32)
    cosT = np.cos(ang.astype(np.float64)).astype(np.float32)
    sinT = np.sin(ang.astype(np.float64)).astype(np.float32)
    ang_s = (np.float32(PAST) * freqs).astype(np.float32)
    cos_s = np.cos(ang_s.astype(np.float64)).astype(np.float32)[None, :]
    sin_s = np.sin(ang_s.astype(np.float64)).astype(np.float32)[None, :]
    lg = np.log1p(-np.exp2(-5.0 - np.arange(H, dtype=np.float64)))
    idx = np.arange(128, dtype=np.float64)
    kdec = np.exp(lg[None, :] * (127.0 - idx[:, None])).astype(np.float32)
    c = np.exp(lg[None, :] * (idx[:, None] + 1.0)) / 16.0
    epsT = (EPS / (c * c)).astype(np.float32)
    j = np.arange(128)[:, None]; i = np.arange(128)[None, :]
    causal = (j <= i).astype(np.float64)
    maskT = np.concatenate([causal * np.exp(-lg[h] * 128.0) for h in range(H)], axis=1).astype(np.float32)
    tril = (np.arange(128)[:, None] >= np.arange(128)[None, :]).astype(np.float32)
    return dict(c_cosT=cosT, c_sinT=sinT, c_cos_s=cos_s, c_sin_s=sin_s, c_kdec=kdec, c_epsT=epsT,
                c_maskT=maskT, c_tril=tril)


_NC_CACHE = {}


def kernel(x_prompt, x_sample, state_ret, p_prompt, p_sample, g_mix, w_in, w_ret_out, ln_a_g, ln_a_b,
           w_s, b_s, w_a_out, b_gate, w_o, g_ffn, w_ff_gate, w_ff_up, w_ff_down, g_ple, w_ple,
           w_ple_gate, g_final):
    f = lambda a: np.ascontiguousarray(np.asarray(a, dtype=np.float32))
    if "nc" not in _NC_CACHE:
        _NC_CACHE["nc"] = build_program()
    nc = _NC_CACHE["nc"]
    consts = _constants()
    shared = dict(g_mix=f(g_mix[0]), w_in=f(w_in[0]), w_ret_out=f(w_ret_out[0]), ln_a_g=f(ln_a_g[0]), ln_a_b=f(ln_a_b[0]),
                  w_s=f(w_s[0]), b_s=f(b_s[0]), w_a_out=f(w_a_out[0]), b_gate=f(b_gate[0]), w_o=f(w_o[0]), g_ffn=f(g_ffn[0]),
                  w_ff_gate=f(w_ff_gate[0]), w_ff_up=f(w_ff_up[0]), w_ff_down=f(w_ff_down[0]), g_ple=f(g_ple[0]),
                  w_ple=f(w_ple[0]), w_ple_gate=f(w_ple_gate[0]), g_final=f(g_final))
    shared.update(consts)
    xp = f(x_prompt); pp = f(p_prompt[0]); xsm = f(x_sample)[:, 0, :]; psm = f(p_sample[0])[:, 0, :]; s0 = f(state_ret[0])
    in_maps = []
    for c in range(8):
        m = dict(shared)
        m["x"] = xp[c]; m["p"] = pp[c]
        m["xs"] = np.ascontiguousarray(xsm[c * NS:(c + 1) * NS]); m["ps"] = np.ascontiguousarray(psm[c * NS:(c + 1) * NS])
        m["s0"] = np.ascontiguousarray(s0[c * NS:(c + 1) * NS])
        in_maps.append(m)
    res = run_bass_kernel_spmd(nc, in_maps, core_ids=list(range(8)))
    r = res.results
    y_prompt = np.stack([r[c]["y"] for c in range(8)], axis=0)
    y_sample = np.concatenate([r[c]["ys"] for c in range(8)], axis=0)[:, None, :]
    st_p = np.stack([r[c]["sp"] for c in range(8)], axis=0)[None]
    st_s = np.concatenate([r[c]["sso"] for c in range(8)], axis=0)[None]
    cv = np.concatenate([r[c]["cv"] for c in range(8)], axis=0)[None, :, None, :]
    return (y_prompt.astype(np.float32), y_sample.astype(np.float32), st_p.astype(np.float32),
            st_s.astype(np.float32), cv.astype(np.float32))
```
